# Optimizing a Trainium2 kernel written in Bass

```python
import jax, jax.numpy as jnp
from jax import lax
import numpy as np

D_MODEL = 1024
BATCH = 2
SEQ = 16384
DEPTH = 2

N_A_LAYERS = DEPTH // 2
N_B_LAYERS = DEPTH - N_A_LAYERS
POOL_WINDOWS = (2, 4, 8, 16)
N_POOL_GROUPS = len(POOL_WINDOWS)
POOL_GROUP = D_MODEL // N_POOL_GROUPS
HEAD_DIM = 128
N_HEADS = D_MODEL // HEAD_DIM
MOBA_BLOCK = 256
MOBA_TOPK = 3
Q_CHUNK = 64
D_FF = ((8 * D_MODEL // 3 + 255) // 256) * 256
ROPE_THETA = 10000.0
EPS = 1e-6
N_SUBLAYERS = 3
N_MOD = 3 * N_SUBLAYERS
MACARON_WEIGHT = 0.5

kernel_name = "yoco_pool_moba_macaron_adaln"


def rms_norm(x, g):
    xf = x.astype(jnp.float32)
    y = xf * lax.rsqrt(jnp.mean(xf * xf, axis=-1, keepdims=True) + EPS)
    return (y * g.astype(jnp.float32)).astype(x.dtype)


def modulate(h, shift, scale):
    return h * (1.0 + scale[:, None, :]) + shift[:, None, :]


def swiglu(h, w_in, w_out):
    gu = h @ w_in
    g, u = jnp.split(gu, 2, axis=-1)
    return (jax.nn.silu(g) * u) @ w_out


def rope_tables(T):
    pos = jnp.arange(T, dtype=jnp.float32)
    inv = ROPE_THETA ** (-jnp.arange(0, HEAD_DIM, 2, dtype=jnp.float32) / HEAD_DIM)
    ang = pos[:, None] * inv[None, :]
    return jnp.cos(ang), jnp.sin(ang)


def rope(x, cos, sin):
    half = HEAD_DIM // 2
    xf = x.astype(jnp.float32)
    x1, x2 = xf[..., :half], xf[..., half:]
    c = cos[None, :, None, :]
    s = sin[None, :, None, :]
    return jnp.concatenate([x1 * c - x2 * s, x2 * c + x1 * s], axis=-1).astype(x.dtype)


def pool_mixer(h, w_pool, pool_scale):
    B, T, _ = h.shape
    hg = h.astype(jnp.float32).reshape(B, T, N_POOL_GROUPS, POOL_GROUP)
    cs = jnp.cumsum(hg, axis=1)
    t = jnp.arange(T)
    outs = []
    for gi, w in enumerate(POOL_WINDOWS):
        c_g = cs[:, :, gi]
        prev = jnp.pad(c_g, ((0, 0), (w, 0), (0, 0)))[:, :T]
        cnt = jnp.minimum(t + 1, w).astype(jnp.float32)[None, :, None]
        outs.append((c_g - prev) / cnt - hg[:, :, gi])
    pooled = jnp.stack(outs, axis=2).astype(h.dtype)
    y = jnp.einsum('btgc,gcd->btgd', pooled, w_pool).reshape(B, T, D_MODEL)
    return y * pool_scale


def shared_kv(h, c_silu, kv_norm, kv_ada_w, kv_ada_b, w_kv, k_norm, cos, sin):
    B, T, _ = h.shape
    shift, scale = jnp.split(c_silu @ kv_ada_w + kv_ada_b, 2, axis=-1)
    hn = modulate(rms_norm(h, kv_norm), shift, scale)
    k, v = jnp.split(hn @ w_kv, 2, axis=-1)
    k = rope(rms_norm(k.reshape(B, T, N_HEADS, HEAD_DIM), k_norm), cos, sin)
    v = v.reshape(B, T, N_HEADS, HEAD_DIM)
    nb = -(-T // MOBA_BLOCK)
    pad = nb * MOBA_BLOCK - T
    k = jnp.pad(k.transpose(0, 2, 1, 3), ((0, 0), (0, 0), (0, pad), (0, 0)))
    v = jnp.pad(v.transpose(0, 2, 1, 3), ((0, 0), (0, 0), (0, pad), (0, 0)))
    k_blocks = k.reshape(B, N_HEADS, nb, MOBA_BLOCK, HEAD_DIM)
    v_blocks = v.reshape(B, N_HEADS, nb, MOBA_BLOCK, HEAD_DIM)
    k_mean = jnp.mean(k_blocks.astype(jnp.float32), axis=3).astype(k.dtype)
    return k_blocks, v_blocks, k_mean


def moba_attention(q, k_blocks, v_blocks, k_mean):
    B, H, T, _ = q.shape
    nb = k_blocks.shape[2]
    topk = min(MOBA_TOPK, nb)
    n_chunks = T // Q_CHUNK
    bi = jnp.arange(B)[:, None, None, None]
    hi = jnp.arange(H)[None, :, None, None]
    sm_scale = HEAD_DIM ** -0.5

    def chunk(ci):
        q0 = ci * Q_CHUNK
        qc = lax.dynamic_slice_in_dim(q, q0, Q_CHUNK, axis=2)
        n = q0 // MOBA_BLOCK
        gate = jnp.einsum('bhqd,bhnd->bhqn', qc, k_mean).astype(jnp.float32)
        gate = jnp.where(jnp.arange(nb) < n, gate, -jnp.inf)
        _, idx = lax.top_k(gate, topk)
        valid = jnp.arange(topk) < n
        k_sel = k_blocks[bi, hi, idx]
        v_sel = v_blocks[bi, hi, idx]
        s_sel = jnp.einsum('bhqd,bhqkjd->bhqkj', qc, k_sel).astype(jnp.float32) * sm_scale
        s_sel = jnp.where(valid[:, None], s_sel, -jnp.inf).reshape(B, H, Q_CHUNK, topk * MOBA_BLOCK)
        k_own = lax.dynamic_index_in_dim(k_blocks, n, axis=2, keepdims=False)
        v_own = lax.dynamic_index_in_dim(v_blocks, n, axis=2, keepdims=False)
        s_own = jnp.einsum('bhqd,bhkd->bhqk', qc, k_own).astype(jnp.float32) * sm_scale
        qpos = q0 + jnp.arange(Q_CHUNK)
        kpos = n * MOBA_BLOCK + jnp.arange(MOBA_BLOCK)
        s_own = jnp.where(kpos[None, :] <= qpos[:, None], s_own, -jnp.inf)
        p = jax.nn.softmax(jnp.concatenate([s_sel, s_own], axis=-1), axis=-1).astype(v_blocks.dtype)
        p_sel = p[..., :topk * MOBA_BLOCK].reshape(B, H, Q_CHUNK, topk, MOBA_BLOCK)
        p_own = p[..., topk * MOBA_BLOCK:]
        return (jnp.einsum('bhqkj,bhqkjd->bhqd', p_sel, v_sel)
                + jnp.einsum('bhqk,bhkd->bhqd', p_own, v_own))

    outs = lax.map(chunk, jnp.arange(n_chunks))
    return outs.transpose(1, 2, 0, 3, 4).reshape(B, H, T, HEAD_DIM)


def setup_inputs(seed: int = 0) -> dict:
    key = jax.random.key(seed)
    ks = jax.random.split(key, 20)
    f32 = jnp.float32
    D = D_MODEL
    nrm = lambda k, shape, s: (jax.random.normal(k, shape, f32) * s)
    return {
        "x": nrm(ks[0], (BATCH, SEQ, D), 1.0),
        "c": nrm(ks[1], (BATCH, D), 1.0),
        "ada_w": nrm(ks[2], (DEPTH, D, N_MOD * D), 0.1 * D ** -0.5),
        "ada_b": nrm(ks[3], (DEPTH, N_MOD * D), 0.05),
        "norm_g": 1.0 + nrm(ks[4], (DEPTH, N_SUBLAYERS, D), 0.05),
        "ffn_w_in": nrm(ks[5], (DEPTH, 2, D, 2 * D_FF), D ** -0.5),
        "ffn_w_out": nrm(ks[6], (DEPTH, 2, D_FF, D), D_FF ** -0.5),
        "pool_w": nrm(ks[7], (N_A_LAYERS, N_POOL_GROUPS, POOL_GROUP, POOL_GROUP), POOL_GROUP ** -0.5),
        "pool_scale": 1.0 + nrm(ks[8], (N_A_LAYERS, D), 0.1),
        "kv_norm": 1.0 + nrm(ks[9], (D,), 0.05),
        "kv_ada_w": nrm(ks[10], (D, 2 * D), 0.1 * D ** -0.5),
        "kv_ada_b": nrm(ks[11], (2 * D,), 0.05),
        "w_kv": nrm(ks[12], (D, 2 * D), D ** -0.5),
        "k_norm": 1.0 + nrm(ks[13], (HEAD_DIM,), 0.05),
        "w_q": nrm(ks[14], (N_B_LAYERS, D, D), D ** -0.5),
        "q_norm": 1.0 + nrm(ks[15], (N_B_LAYERS, HEAD_DIM), 0.05),
        "w_o": nrm(ks[16], (N_B_LAYERS, D, D), D ** -0.5),
    }


def reference(x, c, ada_w, ada_b, norm_g, ffn_w_in, ffn_w_out, pool_w, pool_scale,
              kv_norm, kv_ada_w, kv_ada_b, w_kv, k_norm, w_q, q_norm, w_o):
    B, T, D = x.shape
    c_silu = jax.nn.silu(c)
    cos, sin = rope_tables(T)
    kv = None
    for layer in range(DEPTH):
        mods = jnp.split(c_silu @ ada_w[layer] + ada_b[layer], N_MOD, axis=-1)
        sh, sc, g = mods[0], mods[1], mods[2]
        h = modulate(rms_norm(x, norm_g[layer, 0]), sh, sc)
        x = x + MACARON_WEIGHT * (1.0 + g)[:, None, :] * swiglu(h, ffn_w_in[layer, 0], ffn_w_out[layer, 0])
        sh, sc, g = mods[3], mods[4], mods[5]
        h = modulate(rms_norm(x, norm_g[layer, 1]), sh, sc)
        if layer < N_A_LAYERS:
            y = pool_mixer(h, pool_w[layer], pool_scale[layer])
        else:
            j = layer - N_A_LAYERS
            k_blocks, v_blocks, k_mean = kv
            q = (h @ w_q[j]).reshape(B, T, N_HEADS, HEAD_DIM)
            q = rope(rms_norm(q, q_norm[j]), cos, sin).transpose(0, 2, 1, 3)
            att = moba_attention(q, k_blocks, v_blocks, k_mean)
            y = att.transpose(0, 2, 1, 3).reshape(B, T, D) @ w_o[j]
        x = x + (1.0 + g)[:, None, :] * y
        sh, sc, g = mods[6], mods[7], mods[8]
        h = modulate(rms_norm(x, norm_g[layer, 2]), sh, sc)
        x = x + MACARON_WEIGHT * (1.0 + g)[:, None, :] * swiglu(h, ffn_w_in[layer, 1], ffn_w_out[layer, 1])
        if layer == N_A_LAYERS - 1:
            kv = shared_kv(x, c_silu, kv_norm, kv_ada_w, kv_ada_b, w_kv, k_norm, cos, sin)
    return x
```

```python
import numpy as np
from contextlib import ExitStack

import concourse.bass as bass
import concourse.mybir as mybir
from concourse.bass_utils import run_bass_kernel_spmd

F32 = mybir.dt.float32
BF16 = mybir.dt.bfloat16
AF = mybir.ActivationFunctionType
ALU = mybir.AluOpType

D_MODEL = 1024
BATCH = 2
SEQ = 16384
DEPTH = 2
POOL_WINDOWS = (2, 4, 8, 16)
HEAD_DIM = 128
N_HEADS = D_MODEL // HEAD_DIM
MOBA_BLOCK = 256
MOBA_TOPK = 3
D_FF = 2816
ROPE_THETA = 10000.0
EPS = 1e-6
N_CORES = 8
CORES_PER_SEQ = 4
NBLK = SEQ // MOBA_BLOCK
BLK_PER_CORE = NBLK // CORES_PER_SEQ
TOK_PER_CORE = BLK_PER_CORE * MOBA_BLOCK
HALO = 16


class Buf:
    __slots__ = ("name", "last_w", "readers")

    def __init__(self, name):
        self.name = name
        self.last_w = None
        self.readers = {}


class Op:
    __slots__ = ("eng", "fn", "reads", "writes", "is_dma", "deps", "signal",
                 "ev", "idx")

    def __init__(self, eng, fn, reads, writes, is_dma):
        self.eng = eng
        self.fn = fn
        self.reads = reads
        self.writes = writes
        self.is_dma = is_dma
        self.deps = ()
        self.signal = False
        self.ev = None


SEM_ROT = 12000
DMA_POOL = 12


class Prog:
    def __init__(self, nc, stack, sync_same_engine=True):
        self.nc = nc
        self.stack = stack
        self.ops = []
        self.ptr = 0
        self.sync_same_engine = sync_same_engine
        self.engs = {"pe": nc.tensor, "act": nc.scalar, "dve": nc.vector,
                     "pool": nc.gpsimd, "sp": nc.sync}
        self.final_bufs = []
        self.comp_sems = {}
        self.comp_cnt = {}
        self.n_sem = 0
        self.dma_sems = {}
        self.dma_next = {}
        self.waited = {}
        self.sig_idx = {e: [] for e in self.engs}
        self.last_on_eng = {}
        self.dma_since_barrier = []
        self.pending_barrier = {}
        self.coll_ops = set()

    def op(self, eng, fn, reads=(), writes=()):
        self.ops.append(Op(eng, fn, tuple(reads), tuple(writes), False))

    def dma(self, queue, out, in_, reads=(), writes=(), **kw):
        e = self.engs[queue]
        self.ops.append(Op(queue, lambda: e.dma_start(out=out, in_=in_, **kw),
                           tuple(reads), tuple(writes), True))

    def barrier(self):
        self.ops.append(Op("barrier", None, (), (), False))

    def coll(self, kind, groups, in_ap, out_ap, reads=(), writes=()):
        g = self.nc.gpsimd
        o = Op("pool", lambda: g.collective_compute(kind, ALU.bypass, replica_groups=groups,
                                                     ins=[in_ap], outs=[out_ap]),
               tuple(reads), tuple(writes), True)
        self.ops.append(o)
        self.coll_ops.add(id(o))

    def _new_sem(self, tag):
        self.n_sem += 1
        return self.stack.enter_context(self.nc.semaphore(f"s_{tag}_{self.n_sem}"))

    def _wait(self, engname, sem, val):
        key = (engname, id(sem))
        if self.waited.get(key, 0) >= val:
            return
        self.waited[key] = val
        self.engs[engname].wait_ge(sem, val)

    def _event_of(self, j):
        o = self.ops[j]
        if o.ev is not None:
            return o.ev
        import bisect
        lst = self.sig_idx[o.eng]
        p = bisect.bisect_left(lst, j)
        return self.ops[lst[p]].ev

    def flush(self):
        ops = self.ops
        lo, hi = self.ptr, len(ops)
        self.ptr = hi
        last_of = {}
        for i in range(lo, hi):
            op = ops[i]
            op.idx = i
            if op.eng == "barrier":
                snap = [j for j in self.last_on_eng.values()] + list(self.dma_since_barrier)
                self.dma_since_barrier = []
                for e in self.engs:
                    self.pending_barrier[e] = list(self.pending_barrier.get(e, [])) + snap
                continue
            deps = {}

            def add(j, force=False):
                if j is None:
                    return
                o = ops[j]
                if o.is_dma:
                    deps[("d", j)] = j
                else:
                    if o.eng == op.eng and not op.is_dma:
                        if o.eng == "pe" or not (self.sync_same_engine or force):
                            return
                    k = ("e", o.eng)
                    if k not in deps or deps[k] < j:
                        deps[k] = j

            for j in self.pending_barrier.pop(op.eng, []):
                if not (ops[j].eng == op.eng and not ops[j].is_dma):
                    add(j)
            for b in op.reads:
                add(b.last_w)
            for b in op.writes:
                add(b.last_w)
                for j in b.readers.values():
                    if isinstance(j, list):
                        for jj in j:
                            add(jj)
                    else:
                        add(j)
            for b in op.reads:
                if op.is_dma:
                    b.readers.setdefault("dma", []).append(i)
                else:
                    b.readers[op.eng] = i
            for b in op.writes:
                b.last_w = i
                b.readers = {}
            op.deps = [j for j in deps.values() if j != i]
            for j in op.deps:
                if j >= lo:
                    ops[j].signal = True
            if op.is_dma:
                if id(op) not in self.coll_ops:
                    self.dma_since_barrier.append(i)
            else:
                self.last_on_eng[op.eng] = i
                last_of[op.eng] = i
        for j in last_of.values():
            ops[j].signal = True

        for i in range(lo, hi):
            op = ops[i]
            if op.eng == "barrier":
                continue
            for j in op.deps:
                sem, val = self._event_of(j)
                self._wait(op.eng, sem, val)
            if op.is_dma and id(op) in self.coll_ops:
                sem = self._new_sem("cc")
                ins = op.fn()
                ins.then_inc(sem, 1)
                op.ev = (sem, 1)
            elif op.is_dma:
                q = op.eng
                if q not in self.dma_sems:
                    self.dma_sems[q] = [[self._new_sem("dma" + q), 0] for _ in range(DMA_POOL)]
                    self.dma_next[q] = 0
                slot = self.dma_sems[q][self.dma_next[q] % DMA_POOL]
                self.dma_next[q] += 1
                if slot[1] >= SEM_ROT * 2:
                    slot[0] = self._new_sem("dma" + q)
                    slot[1] = 0
                if slot[1] > 0:
                    self._wait(q, slot[0], slot[1])
                ins = op.fn()
                slot[1] += 16
                ins.then_inc(slot[0], 16)
                op.ev = (slot[0], slot[1])
            else:
                ins = op.fn()
                if op.signal:
                    e = op.eng
                    if e not in self.comp_sems or self.comp_cnt[e] >= SEM_ROT:
                        self.comp_sems[e] = self._new_sem(e)
                        self.comp_cnt[e] = 0
                    self.comp_cnt[e] += 1
                    ins.then_inc(self.comp_sems[e], 1)
                    op.ev = (self.comp_sems[e], self.comp_cnt[e])
                    self.sig_idx[e].append(i)
            op.fn = None

    def finish(self):
        self.flush()
        for op in self.ops:
            if op.is_dma and any(b in self.final_bufs for b in op.writes):
                self._wait("sp", op.ev[0], op.ev[1])
        for e, j in self.last_on_eng.items():
            if e != "sp":
                sem, val = self._event_of(j)
                self._wait("sp", sem, val)


class Ctx:
    pass


def rstd_op(cx, out, out_b, ms_ps, ms_b):
    nc, P = cx.nc, cx.prog
    P.op("act", lambda: nc.scalar.activation(
        out=out[:], in_=ms_ps[:], func=AF.Sqrt, bias=cx.eps_ap[:, 0:1], scale=1.0),
        reads=[ms_b, cx.const_b], writes=[out_b])
    P.op("dve", lambda: nc.vector.reciprocal(out=out[:], in_=out[:]),
         reads=[out_b], writes=[out_b])


class Res:
    def __init__(self, cx, st, tag, N, KC):
        nc = cx.nc
        self.N = N
        self.sq = [st.enter_context(nc.sbuf_tensor(f"{tag}_sq{s}", [128, N], BF16)) for s in range(2)]
        self.sq_b = [Buf(f"sq{s}") for s in range(2)]
        self.tmp = [st.enter_context(nc.sbuf_tensor(f"{tag}_tmp{s}", [128, N], F32)) for s in range(2)]
        self.tmp_b = [Buf(f"tmp{s}") for s in range(2)]
        self.rstd = st.enter_context(nc.sbuf_tensor(f"{tag}_rstd", [128, N], F32))
        self.rstd_b = Buf("rstd")
        self.ss_ps = st.enter_context(nc.psum_tensor(f"{tag}_ss", [128, 512], F32))
        self.ss_b = Buf("ss_ps")


def norm_mod(cx, R, X, Xb, n, KC, gs, sh, h, h_b):
    nc, P = cx.nc, cx.prog
    (gs_ap, gs_b), (sh_ap, sh_b) = gs, sh
    for k in range(KC):
        q = k % 2
        P.op("act", lambda k=k, q=q: nc.scalar.activation(
            out=R.sq[q][:, :n], in_=X[:, k, :n], func=AF.Square),
            reads=[Xb[k]], writes=[R.sq_b[q]])
        P.op("pe", lambda k=k, q=q: nc.tensor.matmul(
            R.ss_ps[:, :n], lhsT=cx.ones_mean[:], rhs=R.sq[q][:, :n],
            start=(k == 0), stop=(k == KC - 1)),
            reads=[R.sq_b[q], cx.const_b], writes=[R.ss_b])
    P.op("act", lambda: nc.scalar.activation(
        out=R.rstd[:, :n], in_=R.ss_ps[:, :n], func=AF.Sqrt, bias=cx.eps_ap[:, 0:1], scale=1.0),
        reads=[R.ss_b, cx.const_b], writes=[R.rstd_b])
    P.op("dve", lambda: nc.vector.reciprocal(out=R.rstd[:, :n], in_=R.rstd[:, :n]),
         reads=[R.rstd_b], writes=[R.rstd_b])
    for k in range(KC):
        q = k % 2
        P.op("dve", lambda k=k, q=q: nc.vector.scalar_tensor_tensor(
            out=R.tmp[q][:, :n], in0=X[:, k, :n], scalar=gs_ap[:, k:k + 1],
            in1=R.rstd[:, :n], op0=ALU.mult, op1=ALU.mult),
            reads=[Xb[k], R.rstd_b, gs_b], writes=[R.tmp_b[q]])
        P.op("pool", lambda k=k, q=q: nc.gpsimd.tensor_scalar(
            out=h[:, k, :n], in0=R.tmp[q][:, :n], scalar1=sh_ap[:, k:k + 1],
            scalar2=None, op0=ALU.add),
            reads=[R.tmp_b[q], sh_b], writes=[h_b[k]])


def tiles_of(segs, N):
    out = []
    for (iv, ov, ntok, ib, ob) in segs:
        t0 = 0
        while t0 < ntok:
            n = min(N, ntok - t0)
            out.append((iv, ov, t0, n, ib, ob))
            t0 += n
    return out


def ffn_phase(cx, segs, w_in, w_out, mod, D, DFF, N=512, tag="f"):
    nc, P = cx.nc, cx.prog
    KC, FC = D // 128, DFF // 128
    mod_ap, mod_b = mod
    gs = (mod_ap[:, 0:KC], mod_b)
    sh = (mod_ap[:, KC:2 * KC], mod_b)
    gh_ap = mod_ap[:, 2 * KC:3 * KC]
    st = ExitStack()
    with st:
        P.barrier()

        def sb(name, shape, dt):
            return st.enter_context(nc.sbuf_tensor(f"{tag}_{name}", shape, dt))

        def ps(name, shape, dt=F32):
            return st.enter_context(nc.psum_tensor(f"{tag}_{name}", shape, dt))

        win = sb("win", [128, KC, 2 * DFF], BF16)
        wout = sb("wout", [128, FC, D], BF16)
        win_b = [Buf(f"win{k}") for k in range(KC)]
        wout_b = [Buf(f"wout{k}") for k in range(FC)]
        XS = 2
        xt = [sb(f"xt{s}", [128, KC, N], F32) for s in range(XS)]
        xt_b = [[Buf(f"xt{s}_{k}") for k in range(KC)] for s in range(XS)]
        h = sb("h", [128, KC, N], BF16)
        h_b = [Buf(f"h{k}") for k in range(KC)]
        act = sb("act", [128, FC, N], BF16)
        act_b = [Buf(f"act{k}") for k in range(FC)]
        sg = [sb(f"sg{s}", [128, N], F32) for s in range(2)]
        sg_b = [Buf(f"sg{s}") for s in range(2)]
        R = Res(cx, st, tag, N, KC)
        g_ps = [ps(f"g{s}", [128, N]) for s in range(2)]
        g_b = [Buf(f"g_ps{s}") for s in range(2)]
        u_ps = [ps(f"u{s}", [128, N]) for s in range(2)]
        u_b = [Buf(f"u_ps{s}") for s in range(2)]
        o_ps = [ps(f"o{s}", [128, N]) for s in range(2)]
        o_b = [Buf(f"o_ps{s}") for s in range(2)]

        w_in_v = w_in.rearrange("(kc p) f -> p kc f", p=128)
        w_out_v = w_out.rearrange("(fc p) d -> p fc d", p=128)
        for k in range(KC):
            P.dma("pool", win[:, k, :], w_in_v[:, k, :], writes=[win_b[k]])
        FG = 2
        for f0 in range(0, FC, FG):
            f1 = min(FC, f0 + FG)
            P.dma("pool", wout[:, f0:f1, :], w_out_v[:, f0:f1, :],
                  writes=wout_b[f0:f1])

        tiles = tiles_of(segs, N)

        def load_x(ti):
            iv, ov, t0, n, ib, ob = tiles[ti]
            s = ti % XS
            P.dma("sp", xt[s][:, :, :n],
                  iv.rearrange("(kc p) n -> p kc n", p=128)[:, :, t0:t0 + n],
                  reads=[ib] if ib else [], writes=xt_b[s])

        load_x(0)
        cnt = 0
        for ti, (iv, ov, t0, n, ib, ob) in enumerate(tiles):
            s = ti % XS
            if ti + 1 < len(tiles):
                load_x(ti + 1)
            X, Xb = xt[s], xt_b[s]
            norm_mod(cx, R, X, Xb, n, KC, gs, sh, h, h_b)
            for f in range(FC):
                q = cnt % 2
                cnt += 1
                for k in range(KC):
                    P.op("pe", lambda f=f, k=k, q=q, n=n: nc.tensor.matmul(
                        g_ps[q][:, :n], lhsT=win[:, k, f * 128:(f + 1) * 128],
                        rhs=h[:, k, :n], start=(k == 0), stop=(k == KC - 1)),
                        reads=[win_b[k], h_b[k]], writes=[g_b[q]])
                for k in range(KC):
                    P.op("pe", lambda f=f, k=k, q=q, n=n: nc.tensor.matmul(
                        u_ps[q][:, :n], lhsT=win[:, k, DFF + f * 128:DFF + (f + 1) * 128],
                        rhs=h[:, k, :n], start=(k == 0), stop=(k == KC - 1)),
                        reads=[win_b[k], h_b[k]], writes=[u_b[q]])
                P.op("act", lambda q=q, n=n: nc.scalar.activation(
                    out=sg[q][:, :n], in_=g_ps[q][:, :n], func=AF.Silu),
                    reads=[g_b[q]], writes=[sg_b[q]])
                P.op("dve", lambda f=f, q=q, n=n: nc.vector.tensor_tensor(
                    out=act[:, f, :n], in0=sg[q][:, :n], in1=u_ps[q][:, :n], op=ALU.mult),
                    reads=[sg_b[q], u_b[q]], writes=[act_b[f]])
            for m in range(KC):
                q = m % 2
                for f in range(FC):
                    P.op("pe", lambda f=f, m=m, q=q, n=n: nc.tensor.matmul(
                        o_ps[q][:, :n], lhsT=wout[:, f, m * 128:(m + 1) * 128],
                        rhs=act[:, f, :n], start=(f == 0), stop=(f == FC - 1)),
                        reads=[wout_b[f], act_b[f]], writes=[o_b[q]])
                P.op("dve", lambda X=X, m=m, q=q, n=n: nc.vector.scalar_tensor_tensor(
                    out=X[:, m, :n], in0=o_ps[q][:, :n], scalar=gh_ap[:, m:m + 1],
                    in1=X[:, m, :n], op0=ALU.mult, op1=ALU.add),
                    reads=[o_b[q], Xb[m], mod_b], writes=[Xb[m]])
            P.dma("sp", ov.rearrange("(kc p) n -> p kc n", p=128)[:, :, t0:t0 + n], X[:, :, :n],
                  reads=Xb, writes=[ob] if ob else [])
        P.flush()


def setup_consts(cx, stack, consts_d):
    nc, P = cx.nc, cx.prog
    cbf = stack.enter_context(nc.sbuf_tensor("c_bf", [128, 3 * 128], BF16))
    epst = stack.enter_context(nc.sbuf_tensor("c_eps", [128, 1], F32))
    cx.const_b = Buf("const")
    P.dma("pool", cbf[:], consts_d[:, 0:384], writes=[cx.const_b])
    P.op("dve", lambda: nc.vector.memset(epst[:], EPS), writes=[cx.const_b])
    cx.ones_mean = cbf[:, 0:128]
    cx.ones_hd = cbf[:, 128:256]
    cx.ident = cbf[:, 256:384]
    cx.eps_ap = epst


def host_consts():
    c = np.zeros((128, 385), np.float32)
    c[:, 0:128] = 1.0 / D_MODEL
    c[:, 128:256] = 1.0 / HEAD_DIM
    c[:, 256:384] = np.eye(128, dtype=np.float32)
    c[:, 384] = EPS
    return c


NMODV = 20
MODD_COLS = 7 * 24


def mods_phase(cx, cT_d, ada_w_d, ada_b_d, kvw_d, ng_d, psc_d, modd, modd_b):
    nc, P = cx.nc, cx.prog
    KC = 8
    st = ExitStack()
    with st:
        P.barrier()

        def sb(name, shape, dt):
            return st.enter_context(nc.sbuf_tensor(f"m_{name}", shape, dt))

        cT = sb("cT", [128, KC], F32)
        cs = sb("cs", [128, KC], BF16)
        bias = sb("bias", [128, NMODV * KC], F32)
        ng = sb("ng", [128, 7 * KC], F32)
        psc = sb("psc", [128, KC], F32)
        modv = sb("modv", [128, NMODV * KC], F32)
        W = [sb(f"W{s}", [128, KC, 1024], BF16) for s in range(2)]
        W_b = [Buf(f"W{s}") for s in range(2)]
        mps_full = st.enter_context(nc.psum_tensor("m_ps", [128, 512], F32))
        mps = mps_full[:, 0:NMODV * KC]
        mps_b = Buf("mps")
        cT_b, cs_b, bias_b, ng_b, psc_b, modv_b = (Buf(n) for n in "cT cs bias ng psc modv".split())
        P.dma("sp", cT[:], cT_d, writes=[cT_b])
        P.dma("sp", bias[:], ada_b_d, writes=[bias_b])
        P.dma("sp", ng[:], ng_d, writes=[ng_b])
        P.dma("sp", psc[:], psc_d, writes=[psc_b])
        P.op("act", lambda: nc.scalar.activation(out=cs[:], in_=cT[:], func=AF.Silu),
             reads=[cT_b], writes=[cs_b])
        for v in range(NMODV):
            s = v % 2
            if v < 18:
                src = ada_w_d[v // 9].rearrange("(kc p) f -> p kc f", p=128)[:, :, (v % 9) * 1024:(v % 9 + 1) * 1024]
            else:
                src = kvw_d.rearrange("(kc p) f -> p kc f", p=128)[:, :, (v - 18) * 1024:(v - 17) * 1024]
            P.dma("pool", W[s][:], src, writes=[W_b[s]])
            for m in range(KC):
                col = v * KC + m
                for k in range(KC):
                    P.op("pe", lambda s=s, m=m, k=k, col=col: nc.tensor.matmul(
                        mps[:, col:col + 1], lhsT=W[s][:, k, m * 128:(m + 1) * 128],
                        rhs=cs[:, k:k + 1], start=(k == 0), stop=(k == KC - 1)),
                        reads=[W_b[s], cs_b], writes=[mps_b])
        P.op("dve", lambda: nc.vector.tensor_tensor(out=modv[:], in0=mps, in1=bias[:], op=ALU.add),
             reads=[mps_b, bias_b], writes=[modv_b])
        for s in range(7):
            if s < 6:
                l, sub = s // 3, s % 3
                v0 = l * 9 + sub * 3
                shv, scv, gv = v0, v0 + 1, v0 + 2
            else:
                shv, scv, gv = 18, 19, None
            c0 = s * 24
            P.op("dve", lambda scv=scv, s=s, c0=c0: nc.vector.scalar_tensor_tensor(
                out=modd[:, c0:c0 + 8], in0=modv[:, scv * 8:scv * 8 + 8], scalar=1.0,
                in1=ng[:, s * 8:s * 8 + 8], op0=ALU.add, op1=ALU.mult),
                reads=[modv_b, ng_b], writes=[modd_b])
            P.op("dve", lambda shv=shv, c0=c0: nc.vector.tensor_copy(
                out=modd[:, c0 + 8:c0 + 16], in_=modv[:, shv * 8:shv * 8 + 8]),
                reads=[modv_b], writes=[modd_b])
            if gv is not None:
                if sub == 1 and l == 0:
                    P.op("dve", lambda gv=gv, c0=c0: nc.vector.scalar_tensor_tensor(
                        out=modd[:, c0 + 16:c0 + 24], in0=modv[:, gv * 8:gv * 8 + 8], scalar=1.0,
                        in1=psc[:], op0=ALU.add, op1=ALU.mult),
                        reads=[modv_b, psc_b], writes=[modd_b])
                else:
                    wgt = 0.5 if sub != 1 else 1.0
                    P.op("dve", lambda gv=gv, c0=c0, wgt=wgt: nc.vector.tensor_scalar(
                        out=modd[:, c0 + 16:c0 + 24], in0=modv[:, gv * 8:gv * 8 + 8], scalar1=1.0,
                        scalar2=wgt, op0=ALU.add, op1=ALU.mult),
                        reads=[modv_b], writes=[modd_b])
            else:
                P.op("dve", lambda c0=c0: nc.vector.memset(modd[:, c0 + 16:c0 + 24], 0.0),
                     writes=[modd_b])
        P.flush()


def pool_phase(cx, x_in, xh_in, x_out, w_pool_d, hmask_d, cnt_d, mod, in_b, inh_b, out_b, tag="p"):
    nc, P = cx.nc, cx.prog
    KC, NB, H = 8, BLK_PER_CORE, HALO
    L = MOBA_BLOCK + H
    mod_ap, mod_b = mod
    gs = (mod_ap[:, 0:KC], mod_b)
    sh = (mod_ap[:, KC:2 * KC], mod_b)
    gp_ap = mod_ap[:, 2 * KC:3 * KC]
    st = ExitStack()
    with st:
        P.barrier()

        def sb(name, shape, dt):
            return st.enter_context(nc.sbuf_tensor(f"{tag}_{name}", shape, dt))

        wp = sb("wp", [128, 4, 2, 256], BF16)
        wp_b = Buf("wp")
        hm = sb("hm", [128, 1], F32)
        ct = sb("ct", [128, KC, H], F32)
        tab_b = Buf("tab")
        xt = [sb(f"xt{s}", [128, KC, L], F32) for s in range(2)]
        xt_b = [[Buf(f"xt{s}_{k}") for k in range(KC)] for s in range(2)]
        hb = sb("h", [128, KC, L], F32)
        hb_b = [Buf(f"h{k}") for k in range(KC)]
        A = [sb(f"A{s}", [128, L], F32) for s in range(2)]
        B = [sb(f"B{s}", [128, L], F32) for s in range(2)]
        A_b = [Buf(f"A{s}") for s in range(2)]
        B_b = [Buf(f"B{s}") for s in range(2)]
        t16 = [sb(f"t16{s}", [128, H], F32) for s in range(2)]
        t16_b = [Buf(f"t16{s}") for s in range(2)]
        pooled = sb("pooled", [128, KC, MOBA_BLOCK], BF16)
        pooled_b = [Buf(f"pooled{k}") for k in range(KC)]
        R = Res(cx, st, tag, L, KC)
        y_ps = [st.enter_context(nc.psum_tensor(f"{tag}_y{s}", [128, 512], F32)) for s in range(2)]
        y_b = [Buf(f"y{s}") for s in range(2)]

        P.dma("pool", wp[:], w_pool_d.rearrange("g (ki p) o -> p g ki o", p=128), writes=[wp_b])
        P.dma("sp", hm[:], hmask_d, writes=[tab_b])
        P.dma("sp", ct[:], cnt_d, writes=[tab_b])
        x_v = x_in.rearrange("(kc p) n -> p kc n", p=128)
        xh_v = xh_in.rearrange("(kc p) n -> p kc n", p=128)
        xo_v = x_out.rearrange("(kc p) n -> p kc n", p=128)

        def load(i):
            s = i % 2
            P.dma("sp", xt[s][:, :, 0:H], xh_v[:, :, i * H:(i + 1) * H],
                  reads=[inh_b] if inh_b else [], writes=xt_b[s])
            P.dma("sp", xt[s][:, :, H:L], x_v[:, :, i * 256:(i + 1) * 256],
                  reads=[in_b] if in_b else [], writes=xt_b[s])

        load(0)
        for i in range(NB):
            s = i % 2
            if i + 1 < NB:
                load(i + 1)
            X, Xb = xt[s], xt_b[s]
            norm_mod(cx, R, X, Xb, L, KC, gs, sh, hb, hb_b)
            if i == 0:
                for k in range(KC):
                    P.op("dve", lambda k=k: nc.vector.tensor_scalar(
                        out=hb[:, k, 0:H], in0=hb[:, k, 0:H], scalar1=hm[:, 0:1], scalar2=None,
                        op0=ALU.mult), reads=[hb_b[k], tab_b], writes=[hb_b[k]])
            for k in range(KC):
                w = POOL_WINDOWS[k // 2]
                nst = {2: 1, 4: 2, 8: 3, 16: 4}[w]
                q = k % 2
                eng = "dve" if k % 2 == 0 else "pool"
                E = nc.vector if k % 2 == 0 else nc.gpsimd
                cur, cur_b = hb[:, k, :], hb_b[k]
                bufs = [(A[q], A_b[q]), (B[q], B_b[q])]
                lo = 0
                for si in range(nst):
                    d = 1 << si
                    lo += d
                    dst, dst_b = bufs[si % 2]
                    P.op(eng, lambda E=E, dst=dst, cur=cur, lo=lo, d=d: E.tensor_tensor(
                        out=dst[:, lo:L], in0=cur[:, lo:L], in1=cur[:, lo - d:L - d], op=ALU.add),
                        reads=[cur_b], writes=[dst_b])
                    cur, cur_b = dst[:, :], dst_b
                P.op("dve", lambda cur=cur, k=k, w=w: nc.vector.scalar_tensor_tensor(
                    out=pooled[:, k, :], in0=cur[:, H:L], scalar=1.0 / w, in1=hb[:, k, H:L],
                    op0=ALU.mult, op1=ALU.subtract),
                    reads=[cur_b, hb_b[k]], writes=[pooled_b[k]])
                if i == 0:
                    P.op("dve", lambda cur=cur, k=k, q=q: nc.vector.tensor_tensor(
                        out=t16[q][:], in0=cur[:, H:2 * H], in1=ct[:, k, :], op=ALU.mult),
                        reads=[cur_b, tab_b], writes=[t16_b[q]])
                    P.op("dve", lambda k=k, q=q: nc.vector.tensor_tensor(
                        out=pooled[:, k, 0:H], in0=t16[q][:], in1=hb[:, k, H:2 * H], op=ALU.subtract),
                        reads=[t16_b[q], hb_b[k]], writes=[pooled_b[k]])
            for g in range(4):
                for mo in range(2):
                    m = 2 * g + mo
                    q = m % 2
                    for ki in range(2):
                        P.op("pe", lambda g=g, mo=mo, ki=ki, q=q: nc.tensor.matmul(
                            y_ps[q][:, 0:MOBA_BLOCK], lhsT=wp[:, g, ki, mo * 128:(mo + 1) * 128],
                            rhs=pooled[:, 2 * g + ki, :], start=(ki == 0), stop=(ki == 1)),
                            reads=[wp_b, pooled_b[2 * g + ki]], writes=[y_b[q]])
                    P.op("dve", lambda X=X, m=m, q=q: nc.vector.scalar_tensor_tensor(
                        out=X[:, m, H:L], in0=y_ps[q][:, 0:MOBA_BLOCK], scalar=gp_ap[:, m:m + 1],
                        in1=X[:, m, H:L], op0=ALU.mult, op1=ALU.add),
                        reads=[y_b[q], Xb[m], mod_b], writes=[Xb[m]])
            P.dma("sp", xo_v[:, :, i * 256:(i + 1) * 256], X[:, :, H:L],
                  reads=Xb, writes=[out_b] if out_b else [])
        P.flush()


class QKRes:
    def __init__(self, cx, st, tag, N):
        nc = cx.nc

        def sb(name, shape, dt):
            return st.enter_context(nc.sbuf_tensor(f"{tag}_{name}", shape, dt))
        self.sqh = [sb(f"sqh{s}", [128, N], BF16) for s in range(2)]
        self.sqh_b = [Buf(f"sqh{s}") for s in range(2)]
        self.rs = [sb(f"rs{s}", [128, N], F32) for s in range(2)]
        self.rs_b = [Buf(f"rs{s}") for s in range(2)]
        self.kn = [sb(f"kn{s}", [128, N], F32) for s in range(2)]
        self.kn_b = [Buf(f"kn{s}") for s in range(2)]
        self.t1 = [sb(f"t1{s}", [128, N], F32) for s in range(2)]
        self.t1_b = [Buf(f"t1{s}") for s in range(2)]
        self.t2 = [sb(f"t2{s}", [128, N], F32) for s in range(2)]
        self.t2_b = [Buf(f"t2{s}") for s in range(2)]
        self.ms_ps = [st.enter_context(nc.psum_tensor(f"{tag}_msh{s}", [128, N], F32)) for s in range(2)]
        self.ms_b = [Buf(f"msh{s}") for s in range(2)]
        self.cnt = 0


def qk_norm_rope(cx, Q, src_ps, src_b, n, gain_ap, gain_b, cos, sin, tab_b, out_ap, out_bs,
                 out32=None, out32_b=None):
    nc, P = cx.nc, cx.prog
    q = Q.cnt % 2
    Q.cnt += 1
    P.op("act", lambda: nc.scalar.activation(out=Q.sqh[q][:, :n], in_=src_ps, func=AF.Square),
         reads=[src_b], writes=[Q.sqh_b[q]])
    P.op("pe", lambda: nc.tensor.matmul(Q.ms_ps[q][:, :n], lhsT=cx.ones_hd, rhs=Q.sqh[q][:, :n],
                                        start=True, stop=True),
         reads=[Q.sqh_b[q], cx.const_b], writes=[Q.ms_b[q]])
    P.op("act", lambda: nc.scalar.activation(out=Q.rs[q][:, :n], in_=Q.ms_ps[q][:, :n], func=AF.Sqrt,
                                             bias=cx.eps_ap[:, 0:1], scale=1.0),
         reads=[Q.ms_b[q], cx.const_b], writes=[Q.rs_b[q]])
    P.op("dve", lambda: nc.vector.reciprocal(out=Q.rs[q][:, :n], in_=Q.rs[q][:, :n]),
         reads=[Q.rs_b[q]], writes=[Q.rs_b[q]])
    P.op("dve", lambda: nc.vector.scalar_tensor_tensor(
        out=Q.kn[q][:, :n], in0=src_ps, scalar=gain_ap, in1=Q.rs[q][:, :n],
        op0=ALU.mult, op1=ALU.mult), reads=[src_b, Q.rs_b[q], gain_b], writes=[Q.kn_b[q]])
    P.op("pool", lambda: nc.gpsimd.tensor_tensor(out=Q.t1[q][:, :n], in0=Q.kn[q][:, :n], in1=cos,
                                                 op=ALU.mult),
         reads=[Q.kn_b[q], tab_b], writes=[Q.t1_b[q]])
    P.op("pool", lambda: nc.gpsimd.tensor_copy(out=Q.t2[q][0:64, :n], in_=Q.kn[q][64:128, :n]),
         reads=[Q.kn_b[q]], writes=[Q.t2_b[q]])
    P.op("dve", lambda: nc.vector.tensor_copy(out=Q.t2[q][64:128, :n], in_=Q.kn[q][0:64, :n]),
         reads=[Q.kn_b[q]], writes=[Q.t2_b[q]])
    P.op("pool", lambda: nc.gpsimd.tensor_tensor(out=Q.t2[q][:, :n], in0=Q.t2[q][:, :n], in1=sin[:],
                                                 op=ALU.mult),
         reads=[Q.t2_b[q], tab_b], writes=[Q.t2_b[q]])
    if out32 is not None:
        P.op("dve", lambda: nc.vector.tensor_tensor(out=out32, in0=Q.t1[q][:, :n], in1=Q.t2[q][:, :n],
                                                    op=ALU.add),
             reads=[Q.t1_b[q], Q.t2_b[q]], writes=[out32_b])
        P.op("pool", lambda: nc.gpsimd.tensor_copy(out=out_ap, in_=out32),
             reads=[out32_b], writes=out_bs)
    else:
        P.op("dve", lambda: nc.vector.tensor_tensor(out=out_ap, in0=Q.t1[q][:, :n], in1=Q.t2[q][:, :n],
                                                    op=ALU.add),
             reads=[Q.t1_b[q], Q.t2_b[q]], writes=out_bs)


def kv_phase(cx, x_in, in_b, w_kv_d, kg_d, cos_d, sin_d, mod, kT_out, v_out, km_out, kv_b, tag="k"):
    nc, P = cx.nc, cx.prog
    KC, N, NH = 8, 512, N_HEADS
    NT = TOK_PER_CORE // N
    mod_ap, mod_b = mod
    gs = (mod_ap[:, 0:KC], mod_b)
    sh = (mod_ap[:, KC:2 * KC], mod_b)
    st = ExitStack()
    with st:
        P.barrier()

        def sb(name, shape, dt):
            return st.enter_context(nc.sbuf_tensor(f"{tag}_{name}", shape, dt))

        wk = sb("wk", [128, KC, 2 * D_MODEL], BF16)
        wk_b = [Buf(f"wk{k}") for k in range(KC)]
        kg = sb("kg", [128, 1], F32)
        kg_b = Buf("kg")
        xt = [sb(f"xt{s}", [128, KC, N], F32) for s in range(2)]
        xt_b = [[Buf(f"xt{s}_{k}") for k in range(KC)] for s in range(2)]
        h = sb("h", [128, KC, N], BF16)
        h_b = [Buf(f"h{k}") for k in range(KC)]
        cs = [sb(f"cos{s}", [128, N], F32) for s in range(2)]
        sn = [sb(f"sin{s}", [128, N], F32) for s in range(2)]
        tab_b = [Buf(f"tab{s}") for s in range(2)]
        kT = [sb(f"kT{s}", [128, NH, N], BF16) for s in range(2)]
        kT_b = [[Buf(f"kT{s}_{m}") for m in range(NH)] for s in range(2)]
        k32 = [sb(f"k32{s}", [128, N], F32) for s in range(2)]
        k32_b = [Buf(f"k32{s}") for s in range(2)]
        km = sb("km", [128, NH, BLK_PER_CORE], F32)
        km_b = Buf("km")
        va = [sb(f"va{s}", [128, NH, HEAD_DIM], BF16) for s in range(2)]
        va_b = [Buf(f"va{s}") for s in range(2)]
        R = Res(cx, st, tag, N, KC)
        Q = QKRes(cx, st, tag, N)
        k_ps = [st.enter_context(nc.psum_tensor(f"{tag}_kps{s}", [128, N], F32)) for s in range(2)]
        k_b = [Buf(f"kps{s}") for s in range(2)]
        v_ps = [st.enter_context(nc.psum_tensor(f"{tag}_vps{s}", [128, N], F32)) for s in range(2)]
        v_b = [Buf(f"vps{s}") for s in range(2)]

        w_v = w_kv_d.rearrange("(kc p) f -> p kc f", p=128)
        for k in range(KC):
            P.dma("pool", wk[:, k, :], w_v[:, k, :], writes=[wk_b[k]])
        P.dma("sp", kg[:], kg_d, writes=[kg_b])
        x_v = x_in.rearrange("(kc p) n -> p kc n", p=128)

        def load(t):
            s = t % 2
            P.dma("sp", xt[s][:], x_v[:, :, t * N:(t + 1) * N], reads=[in_b] if in_b else [],
                  writes=xt_b[s])
            P.dma("sp", cs[s][:], cos_d[:, t * N:(t + 1) * N], writes=[tab_b[s]])
            P.dma("sp", sn[s][:], sin_d[:, t * N:(t + 1) * N], writes=[tab_b[s]])

        load(0)
        vcnt = 0
        for t in range(NT):
            s = t % 2
            if t + 1 < NT:
                load(t + 1)
            X, Xb = xt[s], xt_b[s]
            norm_mod(cx, R, X, Xb, N, KC, gs, sh, h, h_b)
            for m in range(NH):
                q = m % 2
                for k in range(KC):
                    P.op("pe", lambda m=m, k=k, q=q: nc.tensor.matmul(
                        k_ps[q][:], lhsT=wk[:, k, m * 128:(m + 1) * 128], rhs=h[:, k, :],
                        start=(k == 0), stop=(k == KC - 1)),
                        reads=[wk_b[k], h_b[k]], writes=[k_b[q]])
                qk_norm_rope(cx, Q, k_ps[q][:], k_b[q], N, kg[:, 0:1], kg_b, cs[s][:], sn[s], tab_b[s],
                             kT[s][:, m, :], [kT_b[s][m]], out32=k32[q][:], out32_b=k32_b[q])
                for bi in range(2):
                    blk = 2 * t + bi
                    P.op("dve", lambda m=m, q=q, bi=bi, blk=blk: nc.vector.tensor_reduce(
                        out=km[:, m, blk:blk + 1], in_=k32[q][:, bi * 256:(bi + 1) * 256],
                        axis=mybir.AxisListType.X, op=ALU.add),
                        reads=[k32_b[q]], writes=[km_b])
            P.dma("sp", kT_out.rearrange("h p n -> p h n")[:, :, t * N:(t + 1) * N], kT[s][:],
                  reads=kT_b[s], writes=[kv_b] if kv_b else [])
            for ts in range(N // 128):
                vs = vcnt % 2
                vcnt += 1
                for half in range(2):
                    for k in range(KC):
                        P.op("pe", lambda ts=ts, half=half, k=k: nc.tensor.matmul(
                            v_ps[half][:], lhsT=h[:, k, ts * 128:(ts + 1) * 128],
                            rhs=wk[:, k, D_MODEL + half * 512:D_MODEL + (half + 1) * 512],
                            start=(k == 0), stop=(k == KC - 1)),
                            reads=[wk_b[k], h_b[k]], writes=[v_b[half]])
                    P.op("act", lambda half=half, vs=vs: nc.scalar.copy(
                        out=va[vs][:, 4 * half:4 * half + 4, 0:HEAD_DIM],
                        in_=v_ps[half][:].rearrange("p (h d) -> p h d", d=HEAD_DIM)),
                        reads=[v_b[half]], writes=[va_b[vs]])
                lt = t * (N // 128) + ts
                P.dma("sp", v_out[:, :, lt, :].rearrange("h p c -> p h c"), va[vs][:],
                      reads=[va_b[vs]], writes=[kv_b] if kv_b else [])
        P.op("dve", lambda: nc.vector.tensor_scalar(out=km[:], in0=km[:], scalar1=1.0 / MOBA_BLOCK,
                                                    scalar2=None, op0=ALU.mult),
             reads=[km_b], writes=[km_b])
        P.dma("sp", km_out, km[:], reads=[km_b], writes=[kv_b] if kv_b else [])
        P.flush()


NEG = -1.0e30


def attn_phase(cx, x_in, in_b, x_out, out_b, w_q_d, w_o_d, qg_d, cos_d, sin_d, mod,
               kT_g, v_g, km_g, kT_l, v_l, kvg_b, kvl_b, gbias_d, tri_d, tag="a", heads=N_HEADS):
    nc, P = cx.nc, cx.prog
    KC, N, NH = 8, 512, N_HEADS
    NT = TOK_PER_CORE // N
    NB = BLK_PER_CORE
    mod_ap, mod_b = mod
    gs = (mod_ap[:, 0:KC], mod_b)
    sh = (mod_ap[:, KC:2 * KC], mod_b)
    g1_ap = mod_ap[:, 2 * KC:3 * KC]
    SCALE = float(HEAD_DIM) ** -0.5
    outer = ExitStack()
    with outer:
        P.barrier()
        QT = outer.enter_context(nc.sbuf_tensor(f"{tag}_QT", [128, NH, TOK_PER_CORE], BF16))
        QT_b = [[Buf(f"QT{h}_{i}") for i in range(NB)] for h in range(NH)]
        maskS = outer.enter_context(nc.sbuf_tensor(f"{tag}_maskS", [128, 2 * NB, NH, NBLK], BF16))
        maskS_b = [[Buf(f"mS{h}_{i}") for i in range(NB)] for h in range(NH)]
        x_v = x_in.rearrange("(kc p) n -> p kc n", p=128)
        xo_v = x_out.rearrange("(kc p) n -> p kc n", p=128)

        st = ExitStack()
        with st:
            def sb(name, shape, dt):
                return st.enter_context(nc.sbuf_tensor(f"{tag}A_{name}", shape, dt))
            wq = sb("wq", [128, KC, D_MODEL], BF16)
            wq_b = [Buf(f"wq{k}") for k in range(KC)]
            qg = sb("qg", [128, 1], F32)
            qg_b = Buf("qg")
            XA = 1
            xt = [sb(f"xt{s}", [128, KC, N], F32) for s in range(XA)]
            xt_b = [[Buf(f"xt{s}_{k}") for k in range(KC)] for s in range(XA)]
            h = sb("h", [128, KC, N], BF16)
            h_b = [Buf(f"h{k}") for k in range(KC)]
            cs = [sb(f"cos{s}", [128, N], F32) for s in range(2)]
            sn = [sb(f"sin{s}", [128, N], F32) for s in range(2)]
            tab_b = [Buf(f"tab{s}") for s in range(2)]
            R = Res(cx, st, tag + "A", N, KC)
            Q = QKRes(cx, st, tag + "A", N)
            q_ps = [st.enter_context(nc.psum_tensor(f"{tag}A_qps{s}", [128, N], F32)) for s in range(2)]
            q_b = [Buf(f"qps{s}") for s in range(2)]
            q32 = [sb(f"q32{s}", [128, N], F32) for s in range(2)]
            q32_b = [Buf(f"q32{s}") for s in range(2)]
            kml = sb("kml", [128, 4, NH, NB], F32)
            kml_b = Buf("kml")
            kmf = sb("kmf", [128, NH, NB, 4], F32)
            kmf_b = Buf("kmf")
            gbias = sb("gbias", [128, NB, NBLK], F32)
            gbias_b = Buf("gbias")
            gate_ps = [st.enter_context(nc.psum_tensor(f"{tag}A_gate{s}", [128, 512], F32)) for s in range(2)]
            gate_b = [Buf(f"gate{s}") for s in range(2)]
            gbt = [sb(f"gbt{s}", [128, NBLK], F32) for s in range(2)]
            gbt_b = [Buf(f"gbt{s}") for s in range(2)]
            top8 = [sb(f"top8{s}", [128, 8], F32) for s in range(2)]
            top8_b = [Buf(f"top8{s}") for s in range(2)]
            P.dma("sp", gbias[:], gbias_d, writes=[gbias_b])
            for r in range(4):
                P.dma("sp", kml[:, r, :, :], km_g[r], reads=[kvg_b["km"]] if kvg_b else [], writes=[kml_b])
            for r in range(4):
                P.op("dve", lambda r=r: nc.vector.tensor_copy(out=kmf[:, :, :, r], in_=kml[:, r, :, :]),
                     reads=[kml_b], writes=[kmf_b])
            gcnt = 0
            w_v = w_q_d.rearrange("(kc p) f -> p kc f", p=128)
            for k in range(KC):
                P.dma("pool", wq[:, k, :], w_v[:, k, :], writes=[wq_b[k]])
            P.dma("sp", qg[:], qg_d, writes=[qg_b])

            def load(t):
                s = t % 2
                P.dma("sp", xt[t % XA][:], x_v[:, :, t * N:(t + 1) * N], reads=[in_b] if in_b else [],
                      writes=xt_b[t % XA])
                P.dma("sp", cs[s][:], cos_d[:, t * N:(t + 1) * N], writes=[tab_b[s]])
                P.dma("sp", sn[s][:], sin_d[:, t * N:(t + 1) * N], writes=[tab_b[s]])

            load(0)
            for t in range(NT):
                s = t % 2
                X, Xb = xt[t % XA], xt_b[t % XA]
                norm_mod(cx, R, X, Xb, N, KC, gs, sh, h, h_b)
                if t + 1 < NT:
                    load(t + 1)
                for m in range(NH):
                    q = m % 2
                    for k in range(KC):
                        P.op("pe", lambda m=m, k=k, q=q: nc.tensor.matmul(
                            q_ps[q][:], lhsT=wq[:, k, m * 128:(m + 1) * 128], rhs=h[:, k, :],
                            start=(k == 0), stop=(k == KC - 1)),
                            reads=[wq_b[k], h_b[k]], writes=[q_b[q]])
                    qk_norm_rope(cx, Q, q_ps[q][:], q_b[q], N, qg[:, 0:1], qg_b, cs[s][:], sn[s],
                                 tab_b[s], QT[:, m, t * N:(t + 1) * N],
                                 [QT_b[m][2 * t], QT_b[m][2 * t + 1]], out32=q32[q][:], out32_b=q32_b[q])
                    for su in range(4):
                        ug = 4 * t + su
                        i = ug // 2
                        z = gcnt % 2
                        gcnt += 1
                        P.op("pe", lambda m=m, q=q, su=su, z=z: nc.tensor.matmul(
                            gate_ps[z][:, 0:NBLK], lhsT=q32[q][:, su * 128:(su + 1) * 128],
                            rhs=kmf[:, m, :, :].rearrange("p i r -> p (i r)"), start=True, stop=True),
                            reads=[q32_b[q], kmf_b], writes=[gate_b[z]])
                        P.op("dve", lambda z=z, i=i: nc.vector.tensor_tensor(
                            out=gbt[z][:], in0=gate_ps[z][:, 0:NBLK], in1=gbias[:, i, :], op=ALU.add),
                            reads=[gate_b[z], gbias_b], writes=[gbt_b[z]])
                        P.op("dve", lambda z=z: nc.vector.max(out=top8[z][:], in_=gbt[z][:]),
                             reads=[gbt_b[z]], writes=[top8_b[z]])
                        P.op("dve", lambda z=z: nc.vector.tensor_scalar(
                            out=top8[z][:, 2:3], in0=top8[z][:, 2:3], scalar1=-1.0e29, scalar2=None,
                            op0=ALU.max), reads=[top8_b[z]], writes=[top8_b[z]])
                        P.op("dve", lambda z=z, ug=ug, m=m: nc.vector.tensor_scalar(
                            out=maskS[:, ug, m, :], in0=gbt[z][:], scalar1=top8[z][:, 2:3],
                            scalar2=None, op0=ALU.is_ge), reads=[gbt_b[z], top8_b[z]],
                            writes=[maskS_b[m][i]])
            P.flush()

        P.barrier()
        st = ExitStack()
        with st:
            def sb(name, shape, dt):
                return st.enter_context(nc.sbuf_tensor(f"{tag}B_{name}", shape, dt))

            def psum(name, shape, dt=F32):
                return st.enter_context(nc.psum_tensor(f"{tag}B_{name}", shape, dt))
            Kh = sb("Kh", [128, NB, 4, MOBA_BLOCK], BF16)
            Kh_b = Buf("Kh")
            Vh = sb("Vh", [128, NB, 4, 2, HEAD_DIM + 1], BF16)
            Vh_b = Buf("Vh")
            Kl = sb("Kl", [128, TOK_PER_CORE], BF16)
            Kl_b = Buf("Kl")
            Vl = sb("Vl", [128, 2 * NB, HEAD_DIM + 1], BF16)
            Vl_b = Buf("Vl")
            tri = sb("tri", [128, 128], BF16)
            cst_b = Buf("cstB")
            NS = 2
            pt = [sb(f"pt{s}", [128, 2, MOBA_BLOCK], BF16) for s in range(NS)]
            pt_b = [Buf(f"pt{s}") for s in range(NS)]
            sp_ps = [psum(f"sp{s}", [128, 2, MOBA_BLOCK]) for s in range(NS)]
            sp_b = [Buf(f"sp{s}") for s in range(NS)]
            o_ps = [[psum(f"o{s}_{u}", [128, 512]) for u in range(2)] for s in range(NS)]
            o_b = [[Buf(f"o{s}_{u}") for u in range(2)] for s in range(NS)]
            tp_full = psum("tp", [128, 1024], BF16)
            tp_ps = tp_full[:, 0:256].rearrange("p (u q) -> p u q", u=2)
            tp_b = Buf("tp")
            mask = [sb(f"mask{s}", [128, 2, NBLK], F32) for s in range(2)]
            mask_b = [Buf(f"mask{s}") for s in range(2)]
            Oacc = [sb(f"Oacc{s}", [128, 2, HEAD_DIM + 1], F32) for s in range(2)]
            Oacc_b = [[Buf(f"Oacc{s}_{u}") for u in range(2)] for s in range(2)]
            rden = [sb(f"rden{s}", [128, 2], F32) for s in range(2)]
            rden_b = [Buf(f"rden{s}") for s in range(2)]
            obf = [sb(f"obf{s}", [128, 2, 128], BF16) for s in range(2)]
            obf_b = [Buf(f"obf{s}") for s in range(2)]

            P.dma("pool", tri[:], tri_d, writes=[cst_b])
            P.op("dve", lambda: nc.vector.memset(Vh[:].rearrange("p i r s c -> p (i r s) c")[:, :, HEAD_DIM:HEAD_DIM + 1], 1.0),
                 writes=[Vh_b])
            P.op("dve", lambda: nc.vector.memset(Vl[:, :, HEAD_DIM:HEAD_DIM + 1], 1.0), writes=[Vl_b])
            it = 0
            for hh in range(heads):
                for r in range(4):
                    P.dma("sp", Kh[:, :, r, :], kT_g[r, hh].rearrange("p (i t) -> p i t", t=MOBA_BLOCK),
                          reads=[kvg_b["k"][hh]] if kvg_b else [], writes=[Kh_b])
                    for s2 in range(2):
                        P.dma("sp", Vh[:, :, r, s2, 0:HEAD_DIM],
                              v_g[r, hh].rearrange("p (i s) c -> p i s c", s=2)[:, :, s2, :],
                              reads=[kvg_b["v"][hh]] if kvg_b else [], writes=[Vh_b])
                P.dma("sp", Kl[:], kT_l[hh], reads=[kvl_b] if kvl_b else [], writes=[Kl_b])
                for t8 in range(0, 2 * NB, 8):
                    P.dma("sp", Vl[:, t8:t8 + 8, 0:HEAD_DIM], v_l[hh][:, t8:t8 + 8, :],
                          reads=[kvl_b] if kvl_b else [], writes=[Vl_b])
                for i in range(NB):
                    qs = i * MOBA_BLOCK
                    qb = QT_b[hh][i]
                    w = i % 2
                    P.op("dve", lambda w=w, i=i, hh=hh: nc.vector.tensor_copy(
                        out=mask[w][:], in_=maskS[:, 2 * i:2 * i + 2, hh, :]),
                        reads=[maskS_b[hh][i]], writes=[mask_b[w]])
                    z = it % NS
                    it += 1
                    P.op("pe", lambda z=z, qs=qs, hh=hh: nc.tensor.matmul(
                        sp_ps[z][:, 0, :], lhsT=Kl[:, qs:qs + 128], rhs=QT[:, hh, qs:qs + 256],
                        start=True, stop=True), reads=[Kl_b, qb], writes=[sp_b[z]])
                    P.op("pe", lambda z=z, qs=qs, hh=hh: nc.tensor.matmul(
                        sp_ps[z][:, 1, 128:256], lhsT=Kl[:, qs + 128:qs + 256],
                        rhs=QT[:, hh, qs + 128:qs + 256], start=True, stop=True),
                        reads=[Kl_b, qb], writes=[sp_b[z]])
                    P.op("act", lambda z=z: nc.scalar.activation(
                        out=pt[z][:, 0, :], in_=sp_ps[z][:, 0, :], func=AF.Exp, scale=SCALE),
                        reads=[sp_b[z]], writes=[pt_b[z]])
                    P.op("act", lambda z=z: nc.scalar.activation(
                        out=pt[z][:, 1, 128:256], in_=sp_ps[z][:, 1, 128:256], func=AF.Exp, scale=SCALE),
                        reads=[sp_b[z]], writes=[pt_b[z]])
                    P.op("pool", lambda z=z: nc.gpsimd.tensor_tensor(
                        out=pt[z][:, 0, 0:128], in0=pt[z][:, 0, 0:128], in1=tri[:], op=ALU.mult),
                        reads=[pt_b[z], cst_b], writes=[pt_b[z]])
                    P.op("pool", lambda z=z: nc.gpsimd.tensor_tensor(
                        out=pt[z][:, 1, 128:256], in0=pt[z][:, 1, 128:256], in1=tri[:], op=ALU.mult),
                        reads=[pt_b[z], cst_b], writes=[pt_b[z]])
                    P.op("pe", lambda z=z, i=i: nc.tensor.matmul(
                        o_ps[z][0][:, 0:HEAD_DIM + 1], lhsT=pt[z][:, 0, 0:128], rhs=Vl[:, 2 * i, :],
                        start=True, stop=True), reads=[pt_b[z], Vl_b], writes=[o_b[z][0]])
                    P.op("pe", lambda z=z, i=i: nc.tensor.matmul(
                        o_ps[z][1][:, 0:HEAD_DIM + 1], lhsT=pt[z][:, 0, 128:256], rhs=Vl[:, 2 * i, :],
                        start=True, stop=False), reads=[pt_b[z], Vl_b], writes=[o_b[z][1]])
                    P.op("pe", lambda z=z, i=i: nc.tensor.matmul(
                        o_ps[z][1][:, 0:HEAD_DIM + 1], lhsT=pt[z][:, 1, 128:256], rhs=Vl[:, 2 * i + 1, :],
                        start=False, stop=True), reads=[pt_b[z], Vl_b], writes=[o_b[z][1]])
                    for u in range(2):
                        P.op("dve", lambda z=z, u=u, w=w: nc.vector.tensor_copy(
                            out=Oacc[w][:, u, :], in_=o_ps[z][u][:, 0:HEAD_DIM + 1]),
                            reads=[o_b[z][u]], writes=[Oacc_b[w][u]])
                    for blk in range(4 * i + 4):
                        z = it % NS
                        it += 1
                        for a in range(2):
                            P.op("pe", lambda z=z, a=a, blk=blk, qs=qs, hh=hh: nc.tensor.matmul(
                                sp_ps[z][:, a, :], lhsT=Kh[:, blk // 4, blk % 4, a * 128:(a + 1) * 128],
                                rhs=QT[:, hh, qs:qs + 256], start=True, stop=True),
                                reads=[Kh_b, qb], writes=[sp_b[z]])
                        P.op("act", lambda z=z: nc.scalar.activation(
                            out=pt[z][:], in_=sp_ps[z][:], func=AF.Exp, scale=SCALE),
                            reads=[sp_b[z]], writes=[pt_b[z]])
                        for u in range(2):
                            for a in range(2):
                                P.op("pe", lambda z=z, a=a, u=u, blk=blk: nc.tensor.matmul(
                                    o_ps[z][u][:, 0:HEAD_DIM + 1], lhsT=pt[z][:, a, u * 128:(u + 1) * 128],
                                    rhs=Vh[:, blk // 4, blk % 4, a, :], start=(a == 0), stop=(a == 1)),
                                    reads=[pt_b[z], Vh_b], writes=[o_b[z][u]])
                            P.op("dve", lambda z=z, u=u, w=w, blk=blk: nc.vector.scalar_tensor_tensor(
                                out=Oacc[w][:, u, :], in0=o_ps[z][u][:, 0:HEAD_DIM + 1], scalar=mask[w][:, u, blk:blk + 1],
                                in1=Oacc[w][:, u, :], op0=ALU.mult, op1=ALU.add),
                                reads=[o_b[z][u], mask_b[w], Oacc_b[w][u]], writes=[Oacc_b[w][u]])
                    P.op("dve", lambda w=w: nc.vector.reciprocal(out=rden[w][:], in_=Oacc[w][:, :, HEAD_DIM]),
                         reads=Oacc_b[w], writes=[rden_b[w]])
                    for u in range(2):
                        P.op("dve", lambda w=w, u=u: nc.vector.tensor_scalar(
                            out=obf[w][:, u, :], in0=Oacc[w][:, u, 0:HEAD_DIM], scalar1=rden[w][:, u:u + 1],
                            scalar2=None, op0=ALU.mult), reads=[Oacc_b[w][u], rden_b[w]], writes=[obf_b[w]])
                    for u in range(2):
                        P.op("pe", lambda w=w, u=u: nc.tensor.transpose(
                            out=tp_ps[:, u, :], in_=obf[w][:, u, :], identity=cx.ident),
                            reads=[obf_b[w], cx.const_b], writes=[tp_b])
                    P.op("act", lambda qs=qs, hh=hh: nc.scalar.copy(
                        out=QT[:, hh, qs:qs + 256], in_=tp_full[:, 0:256]),
                        reads=[tp_b], writes=[qb])
            P.flush()

        P.barrier()
        st = ExitStack()
        with st:
            def sb(name, shape, dt):
                return st.enter_context(nc.sbuf_tensor(f"{tag}C_{name}", shape, dt))
            wo = sb("wo", [128, KC, D_MODEL], BF16)
            wo_b = [Buf(f"wo{k}") for k in range(KC)]
            xt = [sb(f"xt{s}", [128, KC, N], F32) for s in range(2)]
            xt_b = [[Buf(f"xt{s}_{k}") for k in range(KC)] for s in range(2)]
            y_ps = [st.enter_context(nc.psum_tensor(f"{tag}C_y{s}", [128, N], F32)) for s in range(2)]
            y_b = [Buf(f"y{s}") for s in range(2)]
            w_v = w_o_d.rearrange("(kc p) f -> p kc f", p=128)
            for k in range(KC):
                P.dma("pool", wo[:, k, :], w_v[:, k, :], writes=[wo_b[k]])

            def load(t):
                s = t % 2
                P.dma("sp", xt[s][:], x_v[:, :, t * N:(t + 1) * N], reads=[in_b] if in_b else [],
                      writes=xt_b[s])
            load(0)
            for t in range(NT):
                s = t % 2
                if t + 1 < NT:
                    load(t + 1)
                X, Xb = xt[s], xt_b[s]
                for m in range(KC):
                    q = m % 2
                    for hh in range(NH):
                        P.op("pe", lambda m=m, hh=hh, q=q, t=t: nc.tensor.matmul(
                            y_ps[q][:], lhsT=wo[:, hh, m * 128:(m + 1) * 128],
                            rhs=QT[:, hh, t * N:(t + 1) * N], start=(hh == 0), stop=(hh == NH - 1)),
                            reads=[wo_b[hh], QT_b[hh][2 * t], QT_b[hh][2 * t + 1]], writes=[y_b[q]])
                    P.op("dve", lambda X=X, m=m, q=q: nc.vector.scalar_tensor_tensor(
                        out=X[:, m, :], in0=y_ps[q][:], scalar=g1_ap[:, m:m + 1], in1=X[:, m, :],
                        op0=ALU.mult, op1=ALU.add), reads=[y_b[q], Xb[m], mod_b], writes=[Xb[m]])
                P.dma("sp", xo_v[:, :, t * N:(t + 1) * N], X[:], reads=Xb,
                      writes=[out_b] if out_b else [])
            P.flush()


import ml_dtypes

NPBF16 = ml_dtypes.bfloat16


def _din(nc, name, shape, dt=F32):
    return nc.dram_tensor(name, list(shape), dt, kind="ExternalInput").ap()


def _dout(nc, name, shape, dt=F32):
    return nc.dram_tensor(name, list(shape), dt, kind="ExternalOutput").ap()


def _new():
    nc = bass.Bass("TRN2", target_bir_lowering=False)
    return nc


def _begin(nc, stack):
    P = Prog(nc, stack)
    cx = Ctx()
    cx.nc, cx.prog = nc, P
    consts_d = _din(nc, "consts", [128, 385])
    setup_consts(cx, stack, consts_d)
    return cx, P


def _load_mod(cx, stack, modd_d, s):
    nc, P = cx.nc, cx.prog
    t = stack.enter_context(nc.sbuf_tensor("modd_sb", [128, 24], F32))
    b = Buf("modd")
    P.dma("sp", t[:], modd_d[:, s * 24:(s + 1) * 24], writes=[b])
    return (t, b)


def build_mods():
    nc = _new()
    with ExitStack() as stack:
        cx, P = _begin(nc, stack)
        cT = _din(nc, "cT", [128, 8])
        ada_w = _din(nc, "ada_w", [2, D_MODEL, 9 * D_MODEL])
        ada_b = _din(nc, "ada_b", [128, NMODV * 8])
        kvw = _din(nc, "kvw", [D_MODEL, 2 * D_MODEL])
        ng = _din(nc, "ng", [128, 56])
        psc = _din(nc, "psc", [128, 8])
        out = _dout(nc, "modd", [128, MODD_COLS])
        modd = stack.enter_context(nc.sbuf_tensor("modd_sb", [128, MODD_COLS], F32))
        modd_b = Buf("modd")
        ob = Buf("out")
        P.final_bufs.append(ob)
        mods_phase(cx, cT, ada_w, ada_b, kvw, ng, psc, modd, modd_b)
        P.dma("sp", out, modd[:], reads=[modd_b], writes=[ob])
        P.finish()
    return nc


def build_ffn(s, with_halo):
    nc = _new()
    with ExitStack() as stack:
        cx, P = _begin(nc, stack)
        modd_d = _din(nc, "modd", [128, MODD_COLS])
        mod = _load_mod(cx, stack, modd_d, s)
        x = _din(nc, "x", [D_MODEL, TOK_PER_CORE])
        y = _dout(nc, "y", [D_MODEL, TOK_PER_CORE])
        w_in = _din(nc, "w_in", [D_MODEL, 2 * D_FF])
        w_out = _din(nc, "w_out", [D_FF, D_MODEL])
        ob = Buf("out")
        P.final_bufs.append(ob)
        segs = [(x, y, TOK_PER_CORE, None, ob)]
        if with_halo:
            xh = _din(nc, "xh", [D_MODEL, BLK_PER_CORE * HALO])
            yh = _dout(nc, "yh", [D_MODEL, BLK_PER_CORE * HALO])
            segs.append((xh, yh, BLK_PER_CORE * HALO, None, ob))
        ffn_phase(cx, segs, w_in, w_out, mod, D_MODEL, D_FF)
        P.finish()
    return nc


def build_pool():
    nc = _new()
    with ExitStack() as stack:
        cx, P = _begin(nc, stack)
        modd_d = _din(nc, "modd", [128, MODD_COLS])
        mod = _load_mod(cx, stack, modd_d, 1)
        x = _din(nc, "x", [D_MODEL, TOK_PER_CORE])
        xh = _din(nc, "xh", [D_MODEL, BLK_PER_CORE * HALO])
        y = _dout(nc, "y", [D_MODEL, TOK_PER_CORE])
        wp = _din(nc, "w_pool", [4, 256, 256])
        hm = _din(nc, "hmask", [128, 1])
        ct = _din(nc, "cnt", [128, 8, HALO])
        ob = Buf("out")
        P.final_bufs.append(ob)
        pool_phase(cx, x, xh, y, wp, hm, ct, mod, None, None, ob)
        P.finish()
    return nc


def build_kv():
    nc = _new()
    with ExitStack() as stack:
        cx, P = _begin(nc, stack)
        modd_d = _din(nc, "modd", [128, MODD_COLS])
        mod = _load_mod(cx, stack, modd_d, 6)
        x = _din(nc, "x", [D_MODEL, TOK_PER_CORE])
        w_kv = _din(nc, "w_kv", [D_MODEL, 2 * D_MODEL])
        kg = _din(nc, "kg", [128, 1])
        cos = _din(nc, "cos", [128, TOK_PER_CORE])
        sin = _din(nc, "sin", [128, TOK_PER_CORE])
        kT = _dout(nc, "kT", [N_HEADS, 128, TOK_PER_CORE], BF16)
        v = _dout(nc, "v", [N_HEADS, 128, 2 * BLK_PER_CORE, HEAD_DIM + 1], BF16)
        km = _dout(nc, "km", [128, N_HEADS, BLK_PER_CORE])
        ob = Buf("out")
        P.final_bufs.append(ob)
        kv_phase(cx, x, None, w_kv, kg, cos, sin, mod, kT, v, km, ob)
        P.finish()
    return nc


def build_attn(heads=N_HEADS):
    nc = _new()
    with ExitStack() as stack:
        cx, P = _begin(nc, stack)
        modd_d = _din(nc, "modd", [128, MODD_COLS])
        mod = _load_mod(cx, stack, modd_d, 4)
        x = _din(nc, "x", [D_MODEL, TOK_PER_CORE])
        y = _dout(nc, "y", [D_MODEL, TOK_PER_CORE])
        w_q = _din(nc, "w_q", [D_MODEL, D_MODEL])
        w_o = _din(nc, "w_o", [D_MODEL, D_MODEL])
        qg = _din(nc, "qg", [128, 1])
        cos = _din(nc, "cos", [128, TOK_PER_CORE])
        sin = _din(nc, "sin", [128, TOK_PER_CORE])
        kT_g = _din(nc, "kT_g", [4, N_HEADS, 128, TOK_PER_CORE], BF16)
        v_g = _din(nc, "v_g", [4, N_HEADS, 128, 2 * BLK_PER_CORE, HEAD_DIM + 1], BF16)
        km_g = _din(nc, "km_g", [4, 128, N_HEADS, BLK_PER_CORE])
        kT_l = _din(nc, "kT_l", [N_HEADS, 128, TOK_PER_CORE], BF16)
        v_l = _din(nc, "v_l", [N_HEADS, 128, 2 * BLK_PER_CORE, HEAD_DIM + 1], BF16)
        gbias = _din(nc, "gbias", [128, BLK_PER_CORE, NBLK])
        tri = _din(nc, "tri", [128, 128])
        ob = Buf("out")
        P.final_bufs.append(ob)
        attn_phase(cx, x, None, y, ob, w_q, w_o, qg, cos, sin, mod, kT_g, v_g, km_g, kT_l, v_l,
                   None, None, gbias, tri, heads=heads)
        P.finish()
    return nc


def _fm(v):
    return np.ascontiguousarray(np.asarray(v, np.float32).reshape(8, 128).T)


def core_tokens(c):
    j = c % CORES_PER_SEQ
    blocks = 4 * np.arange(BLK_PER_CORE) + j
    return (blocks[:, None] * MOBA_BLOCK + np.arange(MOBA_BLOCK)[None, :]).reshape(-1)


def host_tables(c):
    j = c % CORES_PER_SEQ
    pos = core_tokens(c).astype(np.float32)
    inv = (np.float32(ROPE_THETA) ** (-(np.arange(0, HEAD_DIM, 2, dtype=np.float32)) / np.float32(HEAD_DIM))).astype(np.float32)
    ang = (pos[None, :] * inv[:, None]).astype(np.float32)
    cos = np.cos(ang).astype(np.float32)
    sin = np.sin(ang).astype(np.float32)
    cosT = np.concatenate([cos, cos], axis=0)
    sinS = np.concatenate([-sin, sin], axis=0)
    gbias = np.zeros((128, BLK_PER_CORE, NBLK), np.float32)
    for i in range(BLK_PER_CORE):
        gbias[:, i, 4 * i + j:] = NEG
    hmask = np.full((128, 1), 0.0 if j == 0 else 1.0, np.float32)
    cnt = np.zeros((128, 8, HALO), np.float32)
    for k in range(8):
        w = POOL_WINDOWS[k // 2]
        for t in range(HALO):
            cnt[:, k, t] = 1.0 / (min(t + 1, w) if j == 0 else w)
    return dict(cos=np.ascontiguousarray(cosT), sin=np.ascontiguousarray(sinS), gbias=gbias,
                hmask=hmask, cnt=cnt)


_NC_CACHE = {}


def _get(name, fn, *a):
    key = (name,) + a
    if key not in _NC_CACHE:
        _NC_CACHE[key] = fn(*a)
    return _NC_CACHE[key]


def _run(nc, in_maps):
    res = run_bass_kernel_spmd(nc, in_maps, core_ids=list(range(N_CORES)))
    return res.results


def kernel_unfused(x, c, ada_w, ada_b, norm_g, ffn_w_in, ffn_w_out, pool_w, pool_scale,
                   kv_norm, kv_ada_w, kv_ada_b, w_kv, k_norm, w_q, q_norm, w_o, _debug=None):
    f32 = np.float32
    x = np.asarray(x, f32)
    c = np.asarray(c, f32)
    ada_w = np.ascontiguousarray(np.asarray(ada_w, f32))
    ada_b = np.asarray(ada_b, f32)
    norm_g = np.asarray(norm_g, f32)
    ffn_w_in = np.asarray(ffn_w_in, f32)
    ffn_w_out = np.asarray(ffn_w_out, f32)
    pool_w = np.ascontiguousarray(np.asarray(pool_w, f32)[0])
    pool_scale = np.asarray(pool_scale, f32)[0]
    kv_norm = np.asarray(kv_norm, f32)
    kv_ada_w = np.ascontiguousarray(np.asarray(kv_ada_w, f32))
    kv_ada_b = np.asarray(kv_ada_b, f32)
    w_kv = np.ascontiguousarray(np.asarray(w_kv, f32))
    k_norm = np.asarray(k_norm, f32)
    w_q = np.ascontiguousarray(np.asarray(w_q, f32)[0])
    q_norm = np.asarray(q_norm, f32)[0]
    w_o = np.ascontiguousarray(np.asarray(w_o, f32)[0])
    consts = host_consts()
    cores = list(range(N_CORES))
    tabs = [host_tables(cc) for cc in cores]
    toks = [core_tokens(cc) for cc in cores]

    ada_b_l = np.concatenate([_fm(ada_b[l, v * 1024:(v + 1) * 1024]) for l in range(2) for v in range(9)]
                             + [_fm(kv_ada_b[v * 1024:(v + 1) * 1024]) for v in range(2)], axis=1)
    ng_l = np.concatenate([_fm(norm_g[l, s]) for l in range(2) for s in range(3)] + [_fm(kv_norm)], axis=1)
    psc_l = _fm(pool_scale)
    xT, xh = [], []
    for cc in cores:
        b, j = cc // 4, cc % 4
        xT.append(np.ascontiguousarray(x[b, toks[cc]].T))
        hal = np.zeros((BLK_PER_CORE * HALO, D_MODEL), f32)
        for i in range(BLK_PER_CORE):
            t0 = (4 * i + j) * MOBA_BLOCK
            if t0 > 0:
                hal[i * HALO:(i + 1) * HALO] = x[b, t0 - HALO:t0]
        xh.append(np.ascontiguousarray(hal.T))

    dbg = {} if _debug is not None else None
    r = _run(_get("mods", build_mods), [dict(consts=consts, cT=_fm(c[cc // 4]), ada_w=ada_w, ada_b=ada_b_l,
                                              kvw=kv_ada_w, ng=ng_l, psc=psc_l) for cc in cores])
    modd = [r[cc]["modd"] for cc in cores]
    wi, wo_ = np.ascontiguousarray(ffn_w_in[0, 0]), np.ascontiguousarray(ffn_w_out[0, 0])
    r = _run(_get("ffn", build_ffn, 0, True), [dict(consts=consts, modd=modd[cc], x=xT[cc], xh=xh[cc],
                                                    w_in=wi, w_out=wo_) for cc in cores])
    x1 = [r[cc]["y"] for cc in cores]
    x1h = [r[cc]["yh"] for cc in cores]
    r = _run(_get("pool", build_pool), [dict(consts=consts, modd=modd[cc], x=x1[cc], xh=x1h[cc], w_pool=pool_w,
                                              hmask=tabs[cc]["hmask"], cnt=tabs[cc]["cnt"]) for cc in cores])
    x2 = [r[cc]["y"] for cc in cores]
    wi, wo_ = np.ascontiguousarray(ffn_w_in[0, 1]), np.ascontiguousarray(ffn_w_out[0, 1])
    r = _run(_get("ffn", build_ffn, 2, False), [dict(consts=consts, modd=modd[cc], x=x2[cc], w_in=wi, w_out=wo_)
                                                 for cc in cores])
    x3 = [r[cc]["y"] for cc in cores]
    kg = np.ascontiguousarray(k_norm.reshape(128, 1))
    r = _run(_get("kv", build_kv), [dict(consts=consts, modd=modd[cc], x=x3[cc], w_kv=w_kv, kg=kg,
                                          cos=tabs[cc]["cos"], sin=tabs[cc]["sin"]) for cc in cores])
    kT_l = [r[cc]["kT"] for cc in cores]
    v_l = [r[cc]["v"] for cc in cores]
    km_l = [r[cc]["km"] for cc in cores]
    wi, wo_ = np.ascontiguousarray(ffn_w_in[1, 0]), np.ascontiguousarray(ffn_w_out[1, 0])
    r = _run(_get("ffn", build_ffn, 3, False), [dict(consts=consts, modd=modd[cc], x=x3[cc], w_in=wi, w_out=wo_)
                                                 for cc in cores])
    x4 = [r[cc]["y"] for cc in cores]
    qg = np.ascontiguousarray(q_norm.reshape(128, 1))
    tri = np.triu(np.ones((128, 128), f32))
    ins = []
    for cc in cores:
        b = cc // 4
        grp = [4 * b + rr for rr in range(4)]
        ins.append(dict(consts=consts, modd=modd[cc], x=x4[cc], w_q=w_q, w_o=w_o, qg=qg,
                        cos=tabs[cc]["cos"], sin=tabs[cc]["sin"],
                        kT_g=np.stack([kT_l[g] for g in grp]), v_g=np.stack([v_l[g] for g in grp]),
                        km_g=np.stack([km_l[g] for g in grp]), kT_l=kT_l[cc], v_l=v_l[cc],
                        gbias=tabs[cc]["gbias"], tri=tri))
    r = _run(_get("attn", build_attn), ins)
    x5 = [r[cc]["y"] for cc in cores]
    wi, wo_ = np.ascontiguousarray(ffn_w_in[1, 1]), np.ascontiguousarray(ffn_w_out[1, 1])
    r = _run(_get("ffn", build_ffn, 5, False), [dict(consts=consts, modd=modd[cc], x=x5[cc], w_in=wi, w_out=wo_)
                                                 for cc in cores])
    x6 = [r[cc]["y"] for cc in cores]
    out = np.empty((BATCH, SEQ, D_MODEL), f32)
    for cc in cores:
        out[cc // 4, toks[cc]] = x6[cc].T
    if _debug is not None:
        _debug.update(modd=modd, x1=x1, x1h=x1h, x2=x2, x3=x3, kT=kT_l, v=v_l, km=km_l, x4=x4, x5=x5, x6=x6,
                      toks=toks)
    return out


GROUPS = [[0, 1, 2, 3], [4, 5, 6, 7]]


def build_fused(stop_after=99):
    nc = _new()
    NH, T, NT128 = N_HEADS, TOK_PER_CORE, 2 * BLK_PER_CORE
    with ExitStack() as stack:
        cx, P = _begin(nc, stack)
        x = _din(nc, "x", [D_MODEL, T])
        xh = _din(nc, "xh", [D_MODEL, BLK_PER_CORE * HALO])
        cT = _din(nc, "cT", [128, 8])
        ada_w = _din(nc, "ada_w", [2, D_MODEL, 9 * D_MODEL])
        ada_b = _din(nc, "ada_b", [128, NMODV * 8])
        kvw = _din(nc, "kvw", [D_MODEL, 2 * D_MODEL])
        ng = _din(nc, "ng", [128, 56])
        psc = _din(nc, "psc", [128, 8])
        w_in = _din(nc, "w_in", [2, 2, D_MODEL, 2 * D_FF])
        w_out = _din(nc, "w_out", [2, 2, D_FF, D_MODEL])
        wp = _din(nc, "w_pool", [4, 256, 256])
        hm = _din(nc, "hmask", [128, 1])
        ct = _din(nc, "cnt", [128, 8, HALO])
        w_kv = _din(nc, "w_kv", [D_MODEL, 2 * D_MODEL])
        kg = _din(nc, "kg", [128, 1])
        cos = _din(nc, "cos", [128, T])
        sin = _din(nc, "sin", [128, T])
        w_q = _din(nc, "w_q", [D_MODEL, D_MODEL])
        w_o = _din(nc, "w_o", [D_MODEL, D_MODEL])
        qg = _din(nc, "qg", [128, 1])
        gbias = _din(nc, "gbias", [128, BLK_PER_CORE, NBLK])
        tri = _din(nc, "tri", [128, 128])
        y = _dout(nc, "y", [D_MODEL, T])

        def scr(name, shape, dt=F32):
            return nc.dram_tensor(name, list(shape), dt, kind="Internal").ap()
        x1 = scr("x1", [D_MODEL, T])
        x1h = scr("x1h", [D_MODEL, BLK_PER_CORE * HALO])
        x2 = scr("x2", [D_MODEL, T])
        x3 = scr("x3", [D_MODEL, T])
        x4 = scr("x4", [D_MODEL, T])
        x5 = scr("x5", [D_MODEL, T])
        kT2 = scr("kT_l", [NH * 128, T], BF16)
        v2 = scr("v_l", [NH * 128, NT128 * HEAD_DIM], BF16)
        km2 = scr("km_l", [128, NH * BLK_PER_CORE])
        kTg2 = scr("kT_g", [NH, 4 * 128, T], BF16)
        vg2 = scr("v_g", [NH, 4 * 128, NT128 * HEAD_DIM], BF16)
        kmg2 = scr("km_g", [4 * 128, NH * BLK_PER_CORE])
        kT_l = kT2.rearrange("(h p) n -> h p n", p=128)
        v_l = v2.rearrange("(h p) (t c) -> h p t c", p=128, c=HEAD_DIM)
        km_l = km2.rearrange("p (h i) -> p h i", i=BLK_PER_CORE)
        kT_g = kTg2.rearrange("h (r p) n -> r h p n", r=4)
        v_g = vg2.rearrange("h (r p) (t c) -> r h p t c", r=4, c=HEAD_DIM)
        km_g = kmg2.rearrange("(r p) (h i) -> r p h i", p=128, i=BLK_PER_CORE)
        b = {n: Buf(n) for n in "x1 x1h x2 x3 x4 x5 kvl out".split()}
        kvg = {"km": Buf("kmg"), "k": [Buf(f"kg{h}") for h in range(NH)],
               "v": [Buf(f"vg{h}") for h in range(NH)]}
        P.final_bufs.append(b["out"])

        modd = stack.enter_context(nc.sbuf_tensor("modd_sb", [128, MODD_COLS], F32))
        modd_b = Buf("modd")

        def mod(s):
            return (modd[:, s * 24:(s + 1) * 24], modd_b)

        def early(src_ap, src_b):
            P.dma("sp", y, src_ap, reads=[src_b], writes=[b["out"]])
            P.finish()

        mods_phase(cx, cT, ada_w, ada_b, kvw, ng, psc, modd, modd_b)
        ffn_phase(cx, [(x, x1, T, None, b["x1"]), (xh, x1h, BLK_PER_CORE * HALO, None, b["x1h"])],
                  w_in[0, 0], w_out[0, 0], mod(0), D_MODEL, D_FF, tag="f0")
        if stop_after == 1:
            early(x1, b["x1"])
            return nc
        pool_phase(cx, x1, x1h, x2, wp, hm, ct, mod(1), b["x1"], b["x1h"], b["x2"])
        ffn_phase(cx, [(x2, x3, T, b["x2"], b["x3"])], w_in[0, 1], w_out[0, 1], mod(2), D_MODEL, D_FF,
                  tag="f1")
        if stop_after == 3:
            early(x3, b["x3"])
            return nc
        kv_phase(cx, x3, b["x3"], w_kv, kg, cos, sin, mod(6), kT_l, v_l, km_l, b["kvl"])
        if stop_after == 4:
            early(x3, b["kvl"])
            return nc
        P.coll("AllGather", GROUPS, km2, kmg2, reads=[b["kvl"]], writes=[kvg["km"]])
        for hh in range(NH):
            P.coll("AllGather", GROUPS, kT2[hh * 128:(hh + 1) * 128, :], kTg2[hh], reads=[b["kvl"]],
                   writes=[kvg["k"][hh]])
            P.coll("AllGather", GROUPS, v2[hh * 128:(hh + 1) * 128, :], vg2[hh], reads=[b["kvl"]],
                   writes=[kvg["v"][hh]])
        ffn_phase(cx, [(x3, x4, T, b["x3"], b["x4"])], w_in[1, 0], w_out[1, 0], mod(3), D_MODEL, D_FF,
                  tag="f2")
        if stop_after == 5:
            P.dma("sp", x5, x4, reads=[b["x4"], kvg["km"]] + kvg["k"] + kvg["v"], writes=[b["x5"]])
            early(x5, b["x5"])
            return nc
        attn_phase(cx, x4, b["x4"], x5, b["x5"], w_q, w_o, qg, cos, sin, mod(4), kT_g, v_g, km_g,
                   kT_l, v_l, kvg, b["kvl"], gbias, tri)
        if stop_after == 6:
            early(x5, b["x5"])
            return nc
        ffn_phase(cx, [(x5, y, T, b["x5"], b["out"])], w_in[1, 1], w_out[1, 1], mod(5), D_MODEL, D_FF,
                  tag="f3")
        P.finish()
    return nc


def kernel(x, c, ada_w, ada_b, norm_g, ffn_w_in, ffn_w_out, pool_w, pool_scale,
           kv_norm, kv_ada_w, kv_ada_b, w_kv, k_norm, w_q, q_norm, w_o, _stop_after=99):
    f32 = np.float32
    x = np.asarray(x, f32)
    c = np.asarray(c, f32)
    ada_w = np.ascontiguousarray(np.asarray(ada_w, f32))
    ada_b = np.asarray(ada_b, f32)
    norm_g = np.asarray(norm_g, f32)
    ffn_w_in = np.ascontiguousarray(np.asarray(ffn_w_in, f32))
    ffn_w_out = np.ascontiguousarray(np.asarray(ffn_w_out, f32))
    pool_w = np.ascontiguousarray(np.asarray(pool_w, f32)[0])
    pool_scale = np.asarray(pool_scale, f32)[0]
    kv_norm = np.asarray(kv_norm, f32)
    kv_ada_w = np.ascontiguousarray(np.asarray(kv_ada_w, f32))
    kv_ada_b = np.asarray(kv_ada_b, f32)
    w_kv = np.ascontiguousarray(np.asarray(w_kv, f32))
    k_norm = np.asarray(k_norm, f32)
    w_q = np.ascontiguousarray(np.asarray(w_q, f32)[0])
    q_norm = np.asarray(q_norm, f32)[0]
    w_o = np.ascontiguousarray(np.asarray(w_o, f32)[0])
    consts = host_consts()
    cores = list(range(N_CORES))
    ada_b_l = np.concatenate([_fm(ada_b[l, v * 1024:(v + 1) * 1024]) for l in range(2) for v in range(9)]
                             + [_fm(kv_ada_b[v * 1024:(v + 1) * 1024]) for v in range(2)], axis=1)
    ng_l = np.concatenate([_fm(norm_g[l, s]) for l in range(2) for s in range(3)] + [_fm(kv_norm)], axis=1)
    psc_l = _fm(pool_scale)
    kg = np.ascontiguousarray(k_norm.reshape(128, 1))
    qg = np.ascontiguousarray(q_norm.reshape(128, 1))
    tri = np.triu(np.ones((128, 128), f32))
    in_maps, toks = [], []
    for cc in cores:
        bb, j = cc // 4, cc % 4
        tk = core_tokens(cc)
        toks.append(tk)
        tabs = host_tables(cc)
        hal = np.zeros((BLK_PER_CORE * HALO, D_MODEL), f32)
        for i in range(BLK_PER_CORE):
            t0 = (4 * i + j) * MOBA_BLOCK
            if t0 > 0:
                hal[i * HALO:(i + 1) * HALO] = x[bb, t0 - HALO:t0]
        in_maps.append(dict(
            consts=consts, x=np.ascontiguousarray(x[bb, tk].T), xh=np.ascontiguousarray(hal.T),
            cT=_fm(c[bb]), ada_w=ada_w, ada_b=ada_b_l, kvw=kv_ada_w, ng=ng_l, psc=psc_l,
            w_in=ffn_w_in, w_out=ffn_w_out, w_pool=pool_w, hmask=tabs["hmask"], cnt=tabs["cnt"],
            w_kv=w_kv, kg=kg, cos=tabs["cos"], sin=tabs["sin"], w_q=w_q, w_o=w_o, qg=qg,
            gbias=tabs["gbias"], tri=tri))
    r = _run(_get("fused", build_fused, _stop_after), in_maps)
    out = np.empty((BATCH, SEQ, D_MODEL), f32)
    for cc in cores:
        out[cc // 4, toks[cc]] = r[cc]["y"].T
    return out
```

```python
import numpy as np
from contextlib import ExitStack

import concourse.bass as bass
import concourse.mybir as mybir
from concourse.bass_utils import run_bass_kernel_spmd

F32 = mybir.dt.float32
BF16 = mybir.dt.bfloat16
AF = mybir.ActivationFunctionType
ALU = mybir.AluOpType

D_MODEL = 1024
BATCH = 2
SEQ = 16384
DEPTH = 2
POOL_WINDOWS = (2, 4, 8, 16)
HEAD_DIM = 128
N_HEADS = D_MODEL // HEAD_DIM
MOBA_BLOCK = 256
MOBA_TOPK = 3
D_FF = 2816
ROPE_THETA = 10000.0
EPS = 1e-6
N_CORES = 8
CORES_PER_SEQ = 4
NBLK = SEQ // MOBA_BLOCK
BLK_PER_CORE = NBLK // CORES_PER_SEQ
TOK_PER_CORE = BLK_PER_CORE * MOBA_BLOCK
HALO = 16


class Buf:
    __slots__ = ("name", "last_w", "readers")

    def __init__(self, name):
        self.name = name
        self.last_w = None
        self.readers = {}


class Op:
    __slots__ = ("eng", "fn", "reads", "writes", "is_dma", "deps", "signal",
                 "ev", "idx")

    def __init__(self, eng, fn, reads, writes, is_dma):
        self.eng = eng
        self.fn = fn
        self.reads = reads
        self.writes = writes
        self.is_dma = is_dma
        self.deps = ()
        self.signal = False
        self.ev = None


SEM_ROT = 12000
DMA_POOL = 12


class Prog:
    def __init__(self, nc, stack, sync_same_engine=True):
        self.nc = nc
        self.stack = stack
        self.ops = []
        self.ptr = 0
        self.sync_same_engine = sync_same_engine
        self.engs = {"pe": nc.tensor, "act": nc.scalar, "dve": nc.vector,
                     "pool": nc.gpsimd, "sp": nc.sync}
        self.final_bufs = []
        self.comp_sems = {}
        self.comp_cnt = {}
        self.n_sem = 0
        self.dma_sems = {}
        self.dma_next = {}
        self.waited = {}
        self.sig_idx = {e: [] for e in self.engs}
        self.last_on_eng = {}
        self.dma_since_barrier = []
        self.pending_barrier = {}
        self.coll_ops = set()

    def op(self, eng, fn, reads=(), writes=()):
        self.ops.append(Op(eng, fn, tuple(reads), tuple(writes), False))

    def dma(self, queue, out, in_, reads=(), writes=(), **kw):
        e = self.engs[queue]
        self.ops.append(Op(queue, lambda: e.dma_start(out=out, in_=in_, **kw),
                           tuple(reads), tuple(writes), True))

    def barrier(self):
        self.ops.append(Op("barrier", None, (), (), False))

    def coll(self, kind, groups, in_ap, out_ap, reads=(), writes=()):
        g = self.nc.gpsimd
        o = Op("pool", lambda: g.collective_compute(kind, ALU.bypass, replica_groups=groups,
                                                     ins=[in_ap], outs=[out_ap]),
               tuple(reads), tuple(writes), True)
        self.ops.append(o)
        self.coll_ops.add(id(o))

    def _new_sem(self, tag):
        self.n_sem += 1
        return self.stack.enter_context(self.nc.semaphore(f"s_{tag}_{self.n_sem}"))

    def _wait(self, engname, sem, val):
        key = (engname, id(sem))
        if self.waited.get(key, 0) >= val:
            return
        self.waited[key] = val
        self.engs[engname].wait_ge(sem, val)

    def _event_of(self, j):
        o = self.ops[j]
        if o.ev is not None:
            return o.ev
        import bisect
        lst = self.sig_idx[o.eng]
        p = bisect.bisect_left(lst, j)
        return self.ops[lst[p]].ev

    def flush(self):
        ops = self.ops
        lo, hi = self.ptr, len(ops)
        self.ptr = hi
        last_of = {}
        for i in range(lo, hi):
            op = ops[i]
            op.idx = i
            if op.eng == "barrier":
                snap = [j for j in self.last_on_eng.values()] + list(self.dma_since_barrier)
                self.dma_since_barrier = []
                for e in self.engs:
                    self.pending_barrier[e] = list(self.pending_barrier.get(e, [])) + snap
                continue
            deps = {}

            def add(j, force=False):
                if j is None:
                    return
                o = ops[j]
                if o.is_dma:
                    deps[("d", j)] = j
                else:
                    if o.eng == op.eng and not op.is_dma:
                        if o.eng == "pe" or not (self.sync_same_engine or force):
                            return
                    k = ("e", o.eng)
                    if k not in deps or deps[k] < j:
                        deps[k] = j

            for j in self.pending_barrier.pop(op.eng, []):
                if not (ops[j].eng == op.eng and not ops[j].is_dma):
                    add(j)
            for b in op.reads:
                add(b.last_w)
            for b in op.writes:
                add(b.last_w)
                for j in b.readers.values():
                    if isinstance(j, list):
                        for jj in j:
                            add(jj)
                    else:
                        add(j)
            for b in op.reads:
                if op.is_dma:
                    b.readers.setdefault("dma", []).append(i)
                else:
                    b.readers[op.eng] = i
            for b in op.writes:
                b.last_w = i
                b.readers = {}
            op.deps = [j for j in deps.values() if j != i]
            for j in op.deps:
                if j >= lo:
                    ops[j].signal = True
            if op.is_dma:
                if id(op) not in self.coll_ops:
                    self.dma_since_barrier.append(i)
            else:
                self.last_on_eng[op.eng] = i
                last_of[op.eng] = i
        for j in last_of.values():
            ops[j].signal = True

        for i in range(lo, hi):
            op = ops[i]
            if op.eng == "barrier":
                continue
            for j in op.deps:
                sem, val = self._event_of(j)
                self._wait(op.eng, sem, val)
            if op.is_dma and id(op) in self.coll_ops:
                sem = self._new_sem("cc")
                ins = op.fn()
                ins.then_inc(sem, 1)
                op.ev = (sem, 1)
            elif op.is_dma:
                q = op.eng
                if q not in self.dma_sems:
                    self.dma_sems[q] = [[self._new_sem("dma" + q), 0] for _ in range(DMA_POOL)]
                    self.dma_next[q] = 0
                slot = self.dma_sems[q][self.dma_next[q] % DMA_POOL]
                self.dma_next[q] += 1
                if slot[1] >= SEM_ROT * 2:
                    slot[0] = self._new_sem("dma" + q)
                    slot[1] = 0
                if slot[1] > 0:
                    self._wait(q, slot[0], slot[1])
                ins = op.fn()
                slot[1] += 16
                ins.then_inc(slot[0], 16)
                op.ev = (slot[0], slot[1])
            else:
                ins = op.fn()
                if op.signal:
                    e = op.eng
                    if e not in self.comp_sems or self.comp_cnt[e] >= SEM_ROT:
                        self.comp_sems[e] = self._new_sem(e)
                        self.comp_cnt[e] = 0
                    self.comp_cnt[e] += 1
                    ins.then_inc(self.comp_sems[e], 1)
                    op.ev = (self.comp_sems[e], self.comp_cnt[e])
                    self.sig_idx[e].append(i)
            op.fn = None

    def finish(self):
        self.flush()
        for op in self.ops:
            if op.is_dma and any(b in self.final_bufs for b in op.writes):
                self._wait("sp", op.ev[0], op.ev[1])
        for e, j in self.last_on_eng.items():
            if e != "sp":
                sem, val = self._event_of(j)
                self._wait("sp", sem, val)


class Ctx:
    pass


def rstd_op(cx, out, out_b, ms_ps, ms_b):
    nc, P = cx.nc, cx.prog
    P.op("act", lambda: nc.scalar.activation(
        out=out[:], in_=ms_ps[:], func=AF.Sqrt, bias=cx.eps_ap[:, 0:1], scale=1.0),
        reads=[ms_b, cx.const_b], writes=[out_b])
    P.op("dve", lambda: nc.vector.reciprocal(out=out[:], in_=out[:]),
         reads=[out_b], writes=[out_b])


class Res:
    def __init__(self, cx, st, tag, N, KC):
        nc = cx.nc
        self.N = N
        self.sq = [st.enter_context(nc.sbuf_tensor(f"{tag}_sq{s}", [128, N], BF16)) for s in range(2)]
        self.sq_b = [Buf(f"sq{s}") for s in range(2)]
        self.tmp = [st.enter_context(nc.sbuf_tensor(f"{tag}_tmp{s}", [128, N], F32)) for s in range(2)]
        self.tmp_b = [Buf(f"tmp{s}") for s in range(2)]
        self.rstd = st.enter_context(nc.sbuf_tensor(f"{tag}_rstd", [128, N], F32))
        self.rstd_b = Buf("rstd")
        self.ss_ps = st.enter_context(nc.psum_tensor(f"{tag}_ss", [128, 512], F32))
        self.ss_b = Buf("ss_ps")


def norm_mod(cx, R, X, Xb, n, KC, gs, sh, h, h_b):
    nc, P = cx.nc, cx.prog
    (gs_ap, gs_b), (sh_ap, sh_b) = gs, sh
    for k in range(KC):
        q = k % 2
        P.op("act", lambda k=k, q=q: nc.scalar.activation(
            out=R.sq[q][:, :n], in_=X[:, k, :n], func=AF.Square),
            reads=[Xb[k]], writes=[R.sq_b[q]])
        P.op("pe", lambda k=k, q=q: nc.tensor.matmul(
            R.ss_ps[:, :n], lhsT=cx.ones_mean[:], rhs=R.sq[q][:, :n],
            start=(k == 0), stop=(k == KC - 1)),
            reads=[R.sq_b[q], cx.const_b], writes=[R.ss_b])
    P.op("act", lambda: nc.scalar.activation(
        out=R.rstd[:, :n], in_=R.ss_ps[:, :n], func=AF.Sqrt, bias=cx.eps_ap[:, 0:1], scale=1.0),
        reads=[R.ss_b, cx.const_b], writes=[R.rstd_b])
    P.op("dve", lambda: nc.vector.reciprocal(out=R.rstd[:, :n], in_=R.rstd[:, :n]),
         reads=[R.rstd_b], writes=[R.rstd_b])
    for k in range(KC):
        q = k % 2
        P.op("dve", lambda k=k, q=q: nc.vector.scalar_tensor_tensor(
            out=R.tmp[q][:, :n], in0=X[:, k, :n], scalar=gs_ap[:, k:k + 1],
            in1=R.rstd[:, :n], op0=ALU.mult, op1=ALU.mult),
            reads=[Xb[k], R.rstd_b, gs_b], writes=[R.tmp_b[q]])
        P.op("pool", lambda k=k, q=q: nc.gpsimd.tensor_scalar(
            out=h[:, k, :n], in0=R.tmp[q][:, :n], scalar1=sh_ap[:, k:k + 1],
            scalar2=None, op0=ALU.add),
            reads=[R.tmp_b[q], sh_b], writes=[h_b[k]])


def tiles_of(segs, N):
    out = []
    for (iv, ov, ntok, ib, ob) in segs:
        t0 = 0
        while t0 < ntok:
            n = min(N, ntok - t0)
            out.append((iv, ov, t0, n, ib, ob))
            t0 += n
    return out


def ffn_phase(cx, segs, w_in, w_out, mod, D, DFF, N=512, tag="f"):
    nc, P = cx.nc, cx.prog
    KC, FC = D // 128, DFF // 128
    mod_ap, mod_b = mod
    gs = (mod_ap[:, 0:KC], mod_b)
    sh = (mod_ap[:, KC:2 * KC], mod_b)
    gh_ap = mod_ap[:, 2 * KC:3 * KC]
    st = ExitStack()
    with st:
        P.barrier()

        def sb(name, shape, dt):
            return st.enter_context(nc.sbuf_tensor(f"{tag}_{name}", shape, dt))

        def ps(name, shape, dt=F32):
            return st.enter_context(nc.psum_tensor(f"{tag}_{name}", shape, dt))

        win = sb("win", [128, KC, 2 * DFF], BF16)
        wout = sb("wout", [128, FC, D], BF16)
        FG = 2
        win_b = [Buf(f"win{g}") for g in range((FC + FG - 1) // FG)]
        wout_b = [Buf(f"wout{k}") for k in range(FC)]
        XS = 2
        xt = [sb(f"xt{s}", [128, KC, N], F32) for s in range(XS)]
        xt_b = [[Buf(f"xt{s}_{k}") for k in range(KC)] for s in range(XS)]
        h = sb("h", [128, KC, N], BF16)
        h_b = [Buf(f"h{k}") for k in range(KC)]
        act = sb("act", [128, FC, N], BF16)
        act_b = [Buf(f"act{k}") for k in range(FC)]
        sg = [sb(f"sg{s}", [128, N], F32) for s in range(2)]
        sg_b = [Buf(f"sg{s}") for s in range(2)]
        R = Res(cx, st, tag, N, KC)
        g_ps = [ps(f"g{s}", [128, N]) for s in range(2)]
        g_b = [Buf(f"g_ps{s}") for s in range(2)]
        u_ps = [ps(f"u{s}", [128, N]) for s in range(2)]
        u_b = [Buf(f"u_ps{s}") for s in range(2)]
        o_ps = [ps(f"o{s}", [128, N]) for s in range(2)]
        o_b = [Buf(f"o_ps{s}") for s in range(2)]

        w_in_v = w_in.rearrange("(kc p) f -> p kc f", p=128)
        w_out_v = w_out.rearrange("(fc p) d -> p fc d", p=128)
        for f0 in range(0, FC, FG):
            f1 = min(FC, f0 + FG)
            for off in (0, DFF):
                P.dma("pool", win[:, :, off + f0 * 128:off + f1 * 128],
                      w_in_v[:, :, off + f0 * 128:off + f1 * 128], writes=[win_b[f0 // FG]])
        for f0 in range(0, FC, FG):
            f1 = min(FC, f0 + FG)
            P.dma("pool", wout[:, f0:f1, :], w_out_v[:, f0:f1, :],
                  writes=wout_b[f0:f1])

        tiles = tiles_of(segs, N)

        def load_x(ti):
            iv, ov, t0, n, ib, ob = tiles[ti]
            s = ti % XS
            P.dma("sp", xt[s][:, :, :n],
                  iv.rearrange("(kc p) n -> p kc n", p=128)[:, :, t0:t0 + n],
                  reads=[ib] if ib else [], writes=xt_b[s])

        load_x(0)
        cnt = 0
        for ti, (iv, ov, t0, n, ib, ob) in enumerate(tiles):
            s = ti % XS
            if ti + 1 < len(tiles):
                load_x(ti + 1)
            X, Xb = xt[s], xt_b[s]
            norm_mod(cx, R, X, Xb, n, KC, gs, sh, h, h_b)
            for f in range(FC):
                q = cnt % 2
                cnt += 1
                for k in range(KC):
                    P.op("pe", lambda f=f, k=k, q=q, n=n: nc.tensor.matmul(
                        g_ps[q][:, :n], lhsT=win[:, k, f * 128:(f + 1) * 128],
                        rhs=h[:, k, :n], start=(k == 0), stop=(k == KC - 1)),
                        reads=[win_b[f // FG], h_b[k]], writes=[g_b[q]])
                for k in range(KC):
                    P.op("pe", lambda f=f, k=k, q=q, n=n: nc.tensor.matmul(
                        u_ps[q][:, :n], lhsT=win[:, k, DFF + f * 128:DFF + (f + 1) * 128],
                        rhs=h[:, k, :n], start=(k == 0), stop=(k == KC - 1)),
                        reads=[win_b[f // FG], h_b[k]], writes=[u_b[q]])
                P.op("act", lambda q=q, n=n: nc.scalar.activation(
                    out=sg[q][:, :n], in_=g_ps[q][:, :n], func=AF.Silu),
                    reads=[g_b[q]], writes=[sg_b[q]])
                P.op("dve", lambda f=f, q=q, n=n: nc.vector.tensor_tensor(
                    out=act[:, f, :n], in0=sg[q][:, :n], in1=u_ps[q][:, :n], op=ALU.mult),
                    reads=[sg_b[q], u_b[q]], writes=[act_b[f]])
            for m in range(KC):
                q = m % 2
                for f in range(FC):
                    P.op("pe", lambda f=f, m=m, q=q, n=n: nc.tensor.matmul(
                        o_ps[q][:, :n], lhsT=wout[:, f, m * 128:(m + 1) * 128],
                        rhs=act[:, f, :n], start=(f == 0), stop=(f == FC - 1)),
                        reads=[wout_b[f], act_b[f]], writes=[o_b[q]])
                P.op("dve", lambda X=X, m=m, q=q, n=n: nc.vector.scalar_tensor_tensor(
                    out=X[:, m, :n], in0=o_ps[q][:, :n], scalar=gh_ap[:, m:m + 1],
                    in1=X[:, m, :n], op0=ALU.mult, op1=ALU.add),
                    reads=[o_b[q], Xb[m], mod_b], writes=[Xb[m]])
            P.dma("sp", ov.rearrange("(kc p) n -> p kc n", p=128)[:, :, t0:t0 + n], X[:, :, :n],
                  reads=Xb, writes=[ob] if ob else [])
        P.flush()


def setup_consts(cx, stack, consts_d):
    nc, P = cx.nc, cx.prog
    cbf = stack.enter_context(nc.sbuf_tensor("c_bf", [128, 3 * 128], BF16))
    epst = stack.enter_context(nc.sbuf_tensor("c_eps", [128, 1], F32))
    cx.const_b = Buf("const")
    P.dma("pool", cbf[:], consts_d[:, 0:384], writes=[cx.const_b])
    P.op("dve", lambda: nc.vector.memset(epst[:], EPS), writes=[cx.const_b])
    cx.ones_mean = cbf[:, 0:128]
    cx.ones_hd = cbf[:, 128:256]
    cx.ident = cbf[:, 256:384]
    cx.eps_ap = epst


def host_consts():
    c = np.zeros((128, 385), np.float32)
    c[:, 0:128] = 1.0 / D_MODEL
    c[:, 128:256] = 1.0 / HEAD_DIM
    c[:, 256:384] = np.eye(128, dtype=np.float32)
    c[:, 384] = EPS
    return c


NMODV = 20
MODD_COLS = 7 * 24


def mods_phase(cx, cT_d, ada_w_d, ada_b_d, kvw_d, ng_d, psc_d, modd, modd_b):
    nc, P = cx.nc, cx.prog
    KC = 8
    st = ExitStack()
    with st:
        P.barrier()

        def sb(name, shape, dt):
            return st.enter_context(nc.sbuf_tensor(f"m_{name}", shape, dt))

        cT = sb("cT", [128, KC], F32)
        cs = sb("cs", [128, KC], BF16)
        bias = sb("bias", [128, NMODV * KC], F32)
        ng = sb("ng", [128, 7 * KC], F32)
        psc = sb("psc", [128, KC], F32)
        modv = sb("modv", [128, NMODV * KC], F32)
        W = [sb(f"W{s}", [128, KC, 1024], BF16) for s in range(2)]
        W_b = [Buf(f"W{s}") for s in range(2)]
        mps_full = st.enter_context(nc.psum_tensor("m_ps", [128, 512], F32))
        mps = mps_full[:, 0:NMODV * KC]
        mps_b = Buf("mps")
        cT_b, cs_b, bias_b, ng_b, psc_b, modv_b = (Buf(n) for n in "cT cs bias ng psc modv".split())
        P.dma("sp", cT[:], cT_d, writes=[cT_b])
        P.dma("sp", bias[:], ada_b_d, writes=[bias_b])
        P.dma("sp", ng[:], ng_d, writes=[ng_b])
        P.dma("sp", psc[:], psc_d, writes=[psc_b])
        P.op("act", lambda: nc.scalar.activation(out=cs[:], in_=cT[:], func=AF.Silu),
             reads=[cT_b], writes=[cs_b])
        for v in range(NMODV):
            s = v % 2
            if v < 18:
                src = ada_w_d[v // 9].rearrange("(kc p) f -> p kc f", p=128)[:, :, (v % 9) * 1024:(v % 9 + 1) * 1024]
            else:
                src = kvw_d.rearrange("(kc p) f -> p kc f", p=128)[:, :, (v - 18) * 1024:(v - 17) * 1024]
            P.dma("pool", W[s][:], src, writes=[W_b[s]])
            for m in range(KC):
                col = v * KC + m
                for k in range(KC):
                    P.op("pe", lambda s=s, m=m, k=k, col=col: nc.tensor.matmul(
                        mps[:, col:col + 1], lhsT=W[s][:, k, m * 128:(m + 1) * 128],
                        rhs=cs[:, k:k + 1], start=(k == 0), stop=(k == KC - 1)),
                        reads=[W_b[s], cs_b], writes=[mps_b])
        P.op("dve", lambda: nc.vector.tensor_tensor(out=modv[:], in0=mps, in1=bias[:], op=ALU.add),
             reads=[mps_b, bias_b], writes=[modv_b])
        for s in range(7):
            if s < 6:
                l, sub = s // 3, s % 3
                v0 = l * 9 + sub * 3
                shv, scv, gv = v0, v0 + 1, v0 + 2
            else:
                shv, scv, gv = 18, 19, None
            c0 = s * 24
            P.op("dve", lambda scv=scv, s=s, c0=c0: nc.vector.scalar_tensor_tensor(
                out=modd[:, c0:c0 + 8], in0=modv[:, scv * 8:scv * 8 + 8], scalar=1.0,
                in1=ng[:, s * 8:s * 8 + 8], op0=ALU.add, op1=ALU.mult),
                reads=[modv_b, ng_b], writes=[modd_b])
            P.op("dve", lambda shv=shv, c0=c0: nc.vector.tensor_copy(
                out=modd[:, c0 + 8:c0 + 16], in_=modv[:, shv * 8:shv * 8 + 8]),
                reads=[modv_b], writes=[modd_b])
            if gv is not None:
                if sub == 1 and l == 0:
                    P.op("dve", lambda gv=gv, c0=c0: nc.vector.scalar_tensor_tensor(
                        out=modd[:, c0 + 16:c0 + 24], in0=modv[:, gv * 8:gv * 8 + 8], scalar=1.0,
                        in1=psc[:], op0=ALU.add, op1=ALU.mult),
                        reads=[modv_b, psc_b], writes=[modd_b])
                else:
                    wgt = 0.5 if sub != 1 else 1.0
                    P.op("dve", lambda gv=gv, c0=c0, wgt=wgt: nc.vector.tensor_scalar(
                        out=modd[:, c0 + 16:c0 + 24], in0=modv[:, gv * 8:gv * 8 + 8], scalar1=1.0,
                        scalar2=wgt, op0=ALU.add, op1=ALU.mult),
                        reads=[modv_b], writes=[modd_b])
            else:
                P.op("dve", lambda c0=c0: nc.vector.memset(modd[:, c0 + 16:c0 + 24], 0.0),
                     writes=[modd_b])
        P.flush()


def pool_phase(cx, x_in, xh_in, x_out, w_pool_d, hmask_d, cnt_d, mod, in_b, inh_b, out_b, tag="p"):
    nc, P = cx.nc, cx.prog
    KC, NB, H = 8, BLK_PER_CORE, HALO
    L = MOBA_BLOCK + H
    mod_ap, mod_b = mod
    gs = (mod_ap[:, 0:KC], mod_b)
    sh = (mod_ap[:, KC:2 * KC], mod_b)
    gp_ap = mod_ap[:, 2 * KC:3 * KC]
    st = ExitStack()
    with st:
        P.barrier()

        def sb(name, shape, dt):
            return st.enter_context(nc.sbuf_tensor(f"{tag}_{name}", shape, dt))

        wp = sb("wp", [128, 4, 2, 256], BF16)
        wp_b = Buf("wp")
        hm = sb("hm", [128, 1], F32)
        ct = sb("ct", [128, KC, H], F32)
        tab_b = Buf("tab")
        xt = [sb(f"xt{s}", [128, KC, L], F32) for s in range(2)]
        xt_b = [[Buf(f"xt{s}_{k}") for k in range(KC)] for s in range(2)]
        hb = sb("h", [128, KC, L], F32)
        hb_b = [Buf(f"h{k}") for k in range(KC)]
        A = [sb(f"A{s}", [128, L], F32) for s in range(2)]
        B = [sb(f"B{s}", [128, L], F32) for s in range(2)]
        A_b = [Buf(f"A{s}") for s in range(2)]
        B_b = [Buf(f"B{s}") for s in range(2)]
        t16 = [sb(f"t16{s}", [128, H], F32) for s in range(2)]
        t16_b = [Buf(f"t16{s}") for s in range(2)]
        pooled = sb("pooled", [128, KC, MOBA_BLOCK], BF16)
        pooled_b = [Buf(f"pooled{k}") for k in range(KC)]
        R = Res(cx, st, tag, L, KC)
        y_ps = [st.enter_context(nc.psum_tensor(f"{tag}_y{s}", [128, 512], F32)) for s in range(2)]
        y_b = [Buf(f"y{s}") for s in range(2)]

        P.dma("pool", wp[:], w_pool_d.rearrange("g (ki p) o -> p g ki o", p=128), writes=[wp_b])
        P.dma("sp", hm[:], hmask_d, writes=[tab_b])
        P.dma("sp", ct[:], cnt_d, writes=[tab_b])
        x_v = x_in.rearrange("(kc p) n -> p kc n", p=128)
        xh_v = xh_in.rearrange("(kc p) n -> p kc n", p=128)
        xo_v = x_out.rearrange("(kc p) n -> p kc n", p=128)

        def load(i):
            s = i % 2
            P.dma("sp", xt[s][:, :, 0:H], xh_v[:, :, i * H:(i + 1) * H],
                  reads=[inh_b] if inh_b else [], writes=xt_b[s])
            P.dma("sp", xt[s][:, :, H:L], x_v[:, :, i * 256:(i + 1) * 256],
                  reads=[in_b] if in_b else [], writes=xt_b[s])

        load(0)
        for i in range(NB):
            s = i % 2
            if i + 1 < NB:
                load(i + 1)
            X, Xb = xt[s], xt_b[s]
            norm_mod(cx, R, X, Xb, L, KC, gs, sh, hb, hb_b)
            if i == 0:
                for k in range(KC):
                    P.op("dve", lambda k=k: nc.vector.tensor_scalar(
                        out=hb[:, k, 0:H], in0=hb[:, k, 0:H], scalar1=hm[:, 0:1], scalar2=None,
                        op0=ALU.mult), reads=[hb_b[k], tab_b], writes=[hb_b[k]])
            for k in range(KC):
                w = POOL_WINDOWS[k // 2]
                nst = {2: 1, 4: 2, 8: 3, 16: 4}[w]
                q = k % 2
                eng = "dve" if k % 2 == 0 else "pool"
                E = nc.vector if k % 2 == 0 else nc.gpsimd
                cur, cur_b = hb[:, k, :], hb_b[k]
                bufs = [(A[q], A_b[q]), (B[q], B_b[q])]
                lo = 0
                for si in range(nst):
                    d = 1 << si
                    lo += d
                    dst, dst_b = bufs[si % 2]
                    P.op(eng, lambda E=E, dst=dst, cur=cur, lo=lo, d=d: E.tensor_tensor(
                        out=dst[:, lo:L], in0=cur[:, lo:L], in1=cur[:, lo - d:L - d], op=ALU.add),
                        reads=[cur_b], writes=[dst_b])
                    cur, cur_b = dst[:, :], dst_b
                P.op("dve", lambda cur=cur, k=k, w=w: nc.vector.scalar_tensor_tensor(
                    out=pooled[:, k, :], in0=cur[:, H:L], scalar=1.0 / w, in1=hb[:, k, H:L],
                    op0=ALU.mult, op1=ALU.subtract),
                    reads=[cur_b, hb_b[k]], writes=[pooled_b[k]])
                if i == 0:
                    P.op("dve", lambda cur=cur, k=k, q=q: nc.vector.tensor_tensor(
                        out=t16[q][:], in0=cur[:, H:2 * H], in1=ct[:, k, :], op=ALU.mult),
                        reads=[cur_b, tab_b], writes=[t16_b[q]])
                    P.op("dve", lambda k=k, q=q: nc.vector.tensor_tensor(
                        out=pooled[:, k, 0:H], in0=t16[q][:], in1=hb[:, k, H:2 * H], op=ALU.subtract),
                        reads=[t16_b[q], hb_b[k]], writes=[pooled_b[k]])
            for g in range(4):
                for mo in range(2):
                    m = 2 * g + mo
                    q = m % 2
                    for ki in range(2):
                        P.op("pe", lambda g=g, mo=mo, ki=ki, q=q: nc.tensor.matmul(
                            y_ps[q][:, 0:MOBA_BLOCK], lhsT=wp[:, g, ki, mo * 128:(mo + 1) * 128],
                            rhs=pooled[:, 2 * g + ki, :], start=(ki == 0), stop=(ki == 1)),
                            reads=[wp_b, pooled_b[2 * g + ki]], writes=[y_b[q]])
                    P.op("dve", lambda X=X, m=m, q=q: nc.vector.scalar_tensor_tensor(
                        out=X[:, m, H:L], in0=y_ps[q][:, 0:MOBA_BLOCK], scalar=gp_ap[:, m:m + 1],
                        in1=X[:, m, H:L], op0=ALU.mult, op1=ALU.add),
                        reads=[y_b[q], Xb[m], mod_b], writes=[Xb[m]])
            P.dma("sp", xo_v[:, :, i * 256:(i + 1) * 256], X[:, :, H:L],
                  reads=Xb, writes=[out_b] if out_b else [])
        P.flush()


class QKRes:
    def __init__(self, cx, st, tag, N):
        nc = cx.nc

        def sb(name, shape, dt):
            return st.enter_context(nc.sbuf_tensor(f"{tag}_{name}", shape, dt))
        self.sqh = [sb(f"sqh{s}", [128, N], BF16) for s in range(2)]
        self.sqh_b = [Buf(f"sqh{s}") for s in range(2)]
        self.rs = [sb(f"rs{s}", [128, N], F32) for s in range(2)]
        self.rs_b = [Buf(f"rs{s}") for s in range(2)]
        self.kn = [sb(f"kn{s}", [128, N], F32) for s in range(2)]
        self.kn_b = [Buf(f"kn{s}") for s in range(2)]
        self.t1 = [sb(f"t1{s}", [128, N], F32) for s in range(2)]
        self.t1_b = [Buf(f"t1{s}") for s in range(2)]
        self.t2 = [sb(f"t2{s}", [128, N], F32) for s in range(2)]
        self.t2_b = [Buf(f"t2{s}") for s in range(2)]
        self.ms_ps = [st.enter_context(nc.psum_tensor(f"{tag}_msh{s}", [128, N], F32)) for s in range(2)]
        self.ms_b = [Buf(f"msh{s}") for s in range(2)]
        self.cnt = 0


def qk_norm_rope(cx, Q, src_ps, src_b, n, gain_ap, gain_b, cos, sin, tab_b, out_ap, out_bs,
                 out32=None, out32_b=None):
    nc, P = cx.nc, cx.prog
    q = Q.cnt % 2
    Q.cnt += 1
    P.op("act", lambda: nc.scalar.activation(out=Q.sqh[q][:, :n], in_=src_ps, func=AF.Square),
         reads=[src_b], writes=[Q.sqh_b[q]])
    P.op("pe", lambda: nc.tensor.matmul(Q.ms_ps[q][:, :n], lhsT=cx.ones_hd, rhs=Q.sqh[q][:, :n],
                                        start=True, stop=True),
         reads=[Q.sqh_b[q], cx.const_b], writes=[Q.ms_b[q]])
    P.op("act", lambda: nc.scalar.activation(out=Q.rs[q][:, :n], in_=Q.ms_ps[q][:, :n], func=AF.Sqrt,
                                             bias=cx.eps_ap[:, 0:1], scale=1.0),
         reads=[Q.ms_b[q], cx.const_b], writes=[Q.rs_b[q]])
    P.op("dve", lambda: nc.vector.reciprocal(out=Q.rs[q][:, :n], in_=Q.rs[q][:, :n]),
         reads=[Q.rs_b[q]], writes=[Q.rs_b[q]])
    P.op("dve", lambda: nc.vector.scalar_tensor_tensor(
        out=Q.kn[q][:, :n], in0=src_ps, scalar=gain_ap, in1=Q.rs[q][:, :n],
        op0=ALU.mult, op1=ALU.mult), reads=[src_b, Q.rs_b[q], gain_b], writes=[Q.kn_b[q]])
    P.op("pool", lambda: nc.gpsimd.tensor_tensor(out=Q.t1[q][:, :n], in0=Q.kn[q][:, :n], in1=cos,
                                                 op=ALU.mult),
         reads=[Q.kn_b[q], tab_b], writes=[Q.t1_b[q]])
    P.op("pool", lambda: nc.gpsimd.tensor_copy(out=Q.t2[q][0:64, :n], in_=Q.kn[q][64:128, :n]),
         reads=[Q.kn_b[q]], writes=[Q.t2_b[q]])
    P.op("dve", lambda: nc.vector.tensor_copy(out=Q.t2[q][64:128, :n], in_=Q.kn[q][0:64, :n]),
         reads=[Q.kn_b[q]], writes=[Q.t2_b[q]])
    P.op("pool", lambda: nc.gpsimd.tensor_tensor(out=Q.t2[q][:, :n], in0=Q.t2[q][:, :n], in1=sin[:],
                                                 op=ALU.mult),
         reads=[Q.t2_b[q], tab_b], writes=[Q.t2_b[q]])
    if out32 is not None:
        P.op("dve", lambda: nc.vector.tensor_tensor(out=out32, in0=Q.t1[q][:, :n], in1=Q.t2[q][:, :n],
                                                    op=ALU.add),
             reads=[Q.t1_b[q], Q.t2_b[q]], writes=[out32_b])
        P.op("pool", lambda: nc.gpsimd.tensor_copy(out=out_ap, in_=out32),
             reads=[out32_b], writes=out_bs)
    else:
        P.op("dve", lambda: nc.vector.tensor_tensor(out=out_ap, in0=Q.t1[q][:, :n], in1=Q.t2[q][:, :n],
                                                    op=ALU.add),
             reads=[Q.t1_b[q], Q.t2_b[q]], writes=out_bs)


def kv_phase(cx, x_in, in_b, w_kv_d, kg_d, cos_d, sin_d, mod, kT_out, v_out, km_out, kv_b, tag="k"):
    nc, P = cx.nc, cx.prog
    KC, N, NH = 8, 512, N_HEADS
    NT = TOK_PER_CORE // N
    mod_ap, mod_b = mod
    gs = (mod_ap[:, 0:KC], mod_b)
    sh = (mod_ap[:, KC:2 * KC], mod_b)
    st = ExitStack()
    with st:
        P.barrier()

        def sb(name, shape, dt):
            return st.enter_context(nc.sbuf_tensor(f"{tag}_{name}", shape, dt))

        wk = sb("wk", [128, KC, 2 * D_MODEL], BF16)
        wk_b = [Buf(f"wk{k}") for k in range(KC)]
        kg = sb("kg", [128, 1], F32)
        kg_b = Buf("kg")
        xt = [sb(f"xt{s}", [128, KC, N], F32) for s in range(2)]
        xt_b = [[Buf(f"xt{s}_{k}") for k in range(KC)] for s in range(2)]
        h = sb("h", [128, KC, N], BF16)
        h_b = [Buf(f"h{k}") for k in range(KC)]
        cs = [sb(f"cos{s}", [128, N], F32) for s in range(2)]
        sn = [sb(f"sin{s}", [128, N], F32) for s in range(2)]
        tab_b = [Buf(f"tab{s}") for s in range(2)]
        kT = [sb(f"kT{s}", [128, NH, N], BF16) for s in range(2)]
        kT_b = [[Buf(f"kT{s}_{m}") for m in range(NH)] for s in range(2)]
        k32 = [sb(f"k32{s}", [128, N], F32) for s in range(2)]
        k32_b = [Buf(f"k32{s}") for s in range(2)]
        km = sb("km", [128, NH, BLK_PER_CORE], F32)
        km_b = Buf("km")
        va = [sb(f"va{s}", [128, NH, HEAD_DIM], BF16) for s in range(2)]
        va_b = [Buf(f"va{s}") for s in range(2)]
        R = Res(cx, st, tag, N, KC)
        Q = QKRes(cx, st, tag, N)
        k_ps = [st.enter_context(nc.psum_tensor(f"{tag}_kps{s}", [128, N], F32)) for s in range(2)]
        k_b = [Buf(f"kps{s}") for s in range(2)]
        v_ps = [st.enter_context(nc.psum_tensor(f"{tag}_vps{s}", [128, N], F32)) for s in range(2)]
        v_b = [Buf(f"vps{s}") for s in range(2)]

        w_v = w_kv_d.rearrange("(kc p) f -> p kc f", p=128)
        for k in range(KC):
            P.dma("pool", wk[:, k, :], w_v[:, k, :], writes=[wk_b[k]])
        P.dma("sp", kg[:], kg_d, writes=[kg_b])
        x_v = x_in.rearrange("(kc p) n -> p kc n", p=128)

        def load(t):
            s = t % 2
            P.dma("sp", xt[s][:], x_v[:, :, t * N:(t + 1) * N], reads=[in_b] if in_b else [],
                  writes=xt_b[s])
            P.dma("sp", cs[s][:], cos_d[:, t * N:(t + 1) * N], writes=[tab_b[s]])
            P.dma("sp", sn[s][:], sin_d[:, t * N:(t + 1) * N], writes=[tab_b[s]])

        load(0)
        vcnt = 0
        for t in range(NT):
            s = t % 2
            if t + 1 < NT:
                load(t + 1)
            X, Xb = xt[s], xt_b[s]
            norm_mod(cx, R, X, Xb, N, KC, gs, sh, h, h_b)
            for m in range(NH):
                q = m % 2
                for k in range(KC):
                    P.op("pe", lambda m=m, k=k, q=q: nc.tensor.matmul(
                        k_ps[q][:], lhsT=wk[:, k, m * 128:(m + 1) * 128], rhs=h[:, k, :],
                        start=(k == 0), stop=(k == KC - 1)),
                        reads=[wk_b[k], h_b[k]], writes=[k_b[q]])
                qk_norm_rope(cx, Q, k_ps[q][:], k_b[q], N, kg[:, 0:1], kg_b, cs[s][:], sn[s], tab_b[s],
                             kT[s][:, m, :], [kT_b[s][m]], out32=k32[q][:], out32_b=k32_b[q])
                for bi in range(2):
                    blk = 2 * t + bi
                    P.op("dve", lambda m=m, q=q, bi=bi, blk=blk: nc.vector.tensor_reduce(
                        out=km[:, m, blk:blk + 1], in_=k32[q][:, bi * 256:(bi + 1) * 256],
                        axis=mybir.AxisListType.X, op=ALU.add),
                        reads=[k32_b[q]], writes=[km_b])
            P.dma("sp", kT_out.rearrange("h p n -> p h n")[:, :, t * N:(t + 1) * N], kT[s][:],
                  reads=kT_b[s], writes=[kv_b] if kv_b else [])
            for ts in range(N // 128):
                vs = vcnt % 2
                vcnt += 1
                for half in range(2):
                    for k in range(KC):
                        P.op("pe", lambda ts=ts, half=half, k=k: nc.tensor.matmul(
                            v_ps[half][:], lhsT=h[:, k, ts * 128:(ts + 1) * 128],
                            rhs=wk[:, k, D_MODEL + half * 512:D_MODEL + (half + 1) * 512],
                            start=(k == 0), stop=(k == KC - 1)),
                            reads=[wk_b[k], h_b[k]], writes=[v_b[half]])
                    P.op("act", lambda half=half, vs=vs: nc.scalar.copy(
                        out=va[vs][:, 4 * half:4 * half + 4, 0:HEAD_DIM],
                        in_=v_ps[half][:].rearrange("p (h d) -> p h d", d=HEAD_DIM)),
                        reads=[v_b[half]], writes=[va_b[vs]])
                lt = t * (N // 128) + ts
                P.dma("sp", v_out[:, :, lt, :].rearrange("h p c -> p h c"), va[vs][:],
                      reads=[va_b[vs]], writes=[kv_b] if kv_b else [])
        P.op("dve", lambda: nc.vector.tensor_scalar(out=km[:], in0=km[:], scalar1=1.0 / MOBA_BLOCK,
                                                    scalar2=None, op0=ALU.mult),
             reads=[km_b], writes=[km_b])
        P.dma("sp", km_out, km[:], reads=[km_b], writes=[kv_b] if kv_b else [])
        P.flush()


NEG = -1.0e30


def attn_phase(cx, x_in, in_b, x_out, out_b, w_q_d, w_o_d, qg_d, cos_d, sin_d, mod,
               kT_g, v_g, km_g, kT_l, v_l, kvg_b, kvl_b, gbias_d, tri_d, tag="a", heads=N_HEADS):
    nc, P = cx.nc, cx.prog
    KC, N, NH = 8, 512, N_HEADS
    NT = TOK_PER_CORE // N
    NB = BLK_PER_CORE
    mod_ap, mod_b = mod
    gs = (mod_ap[:, 0:KC], mod_b)
    sh = (mod_ap[:, KC:2 * KC], mod_b)
    g1_ap = mod_ap[:, 2 * KC:3 * KC]
    SCALE = float(HEAD_DIM) ** -0.5
    outer = ExitStack()
    with outer:
        P.barrier()
        QT = outer.enter_context(nc.sbuf_tensor(f"{tag}_QT", [128, NH, TOK_PER_CORE], BF16))
        QT_b = [[Buf(f"QT{h}_{i}") for i in range(NB)] for h in range(NH)]
        maskS = outer.enter_context(nc.sbuf_tensor(f"{tag}_maskS", [128, 2 * NB, NH, NBLK], BF16))
        maskS_b = [[Buf(f"mS{h}_{i}") for i in range(NB)] for h in range(NH)]
        x_v = x_in.rearrange("(kc p) n -> p kc n", p=128)
        xo_v = x_out.rearrange("(kc p) n -> p kc n", p=128)

        st = ExitStack()
        with st:
            def sb(name, shape, dt):
                return st.enter_context(nc.sbuf_tensor(f"{tag}A_{name}", shape, dt))
            wq = sb("wq", [128, KC, D_MODEL], BF16)
            wq_b = [Buf(f"wq{k}") for k in range(KC)]
            qg = sb("qg", [128, 1], F32)
            qg_b = Buf("qg")
            XA = 1
            xt = [sb(f"xt{s}", [128, KC, N], F32) for s in range(XA)]
            xt_b = [[Buf(f"xt{s}_{k}") for k in range(KC)] for s in range(XA)]
            h = sb("h", [128, KC, N], BF16)
            h_b = [Buf(f"h{k}") for k in range(KC)]
            cs = [sb(f"cos{s}", [128, N], F32) for s in range(2)]
            sn = [sb(f"sin{s}", [128, N], F32) for s in range(2)]
            tab_b = [Buf(f"tab{s}") for s in range(2)]
            R = Res(cx, st, tag + "A", N, KC)
            Q = QKRes(cx, st, tag + "A", N)
            q_ps = [st.enter_context(nc.psum_tensor(f"{tag}A_qps{s}", [128, N], F32)) for s in range(2)]
            q_b = [Buf(f"qps{s}") for s in range(2)]
            q32 = [sb(f"q32{s}", [128, N], F32) for s in range(2)]
            q32_b = [Buf(f"q32{s}") for s in range(2)]
            kml = sb("kml", [128, 4, NH, NB], F32)
            kml_b = Buf("kml")
            kmf = sb("kmf", [128, NH, NB, 4], F32)
            kmf_b = Buf("kmf")
            gbias = sb("gbias", [128, NB, NBLK], F32)
            gbias_b = Buf("gbias")
            gate_ps = [st.enter_context(nc.psum_tensor(f"{tag}A_gate{s}", [128, 512], F32)) for s in range(2)]
            gate_b = [Buf(f"gate{s}") for s in range(2)]
            gbt = [sb(f"gbt{s}", [128, NBLK], F32) for s in range(2)]
            gbt_b = [Buf(f"gbt{s}") for s in range(2)]
            top8 = [sb(f"top8{s}", [128, 8], F32) for s in range(2)]
            top8_b = [Buf(f"top8{s}") for s in range(2)]
            P.dma("sp", gbias[:], gbias_d, writes=[gbias_b])
            for r in range(4):
                P.dma("sp", kml[:, r, :, :], km_g[r], reads=[kvg_b["km"]] if kvg_b else [], writes=[kml_b])
            for r in range(4):
                P.op("dve", lambda r=r: nc.vector.tensor_copy(out=kmf[:, :, :, r], in_=kml[:, r, :, :]),
                     reads=[kml_b], writes=[kmf_b])
            gcnt = 0
            w_v = w_q_d.rearrange("(kc p) f -> p kc f", p=128)
            for k in range(KC):
                P.dma("pool", wq[:, k, :], w_v[:, k, :], writes=[wq_b[k]])
            P.dma("sp", qg[:], qg_d, writes=[qg_b])

            def load(t):
                s = t % 2
                P.dma("sp", xt[t % XA][:], x_v[:, :, t * N:(t + 1) * N], reads=[in_b] if in_b else [],
                      writes=xt_b[t % XA])
                P.dma("sp", cs[s][:], cos_d[:, t * N:(t + 1) * N], writes=[tab_b[s]])
                P.dma("sp", sn[s][:], sin_d[:, t * N:(t + 1) * N], writes=[tab_b[s]])

            load(0)
            for t in range(NT):
                s = t % 2
                X, Xb = xt[t % XA], xt_b[t % XA]
                norm_mod(cx, R, X, Xb, N, KC, gs, sh, h, h_b)
                if t + 1 < NT:
                    load(t + 1)
                for m in range(NH):
                    q = m % 2
                    for k in range(KC):
                        P.op("pe", lambda m=m, k=k, q=q: nc.tensor.matmul(
                            q_ps[q][:], lhsT=wq[:, k, m * 128:(m + 1) * 128], rhs=h[:, k, :],
                            start=(k == 0), stop=(k == KC - 1)),
                            reads=[wq_b[k], h_b[k]], writes=[q_b[q]])
                    qk_norm_rope(cx, Q, q_ps[q][:], q_b[q], N, qg[:, 0:1], qg_b, cs[s][:], sn[s],
                                 tab_b[s], QT[:, m, t * N:(t + 1) * N],
                                 [QT_b[m][2 * t], QT_b[m][2 * t + 1]], out32=q32[q][:], out32_b=q32_b[q])
                    for su in range(4):
                        ug = 4 * t + su
                        i = ug // 2
                        z = gcnt % 2
                        gcnt += 1
                        P.op("pe", lambda m=m, q=q, su=su, z=z: nc.tensor.matmul(
                            gate_ps[z][:, 0:NBLK], lhsT=q32[q][:, su * 128:(su + 1) * 128],
                            rhs=kmf[:, m, :, :].rearrange("p i r -> p (i r)"), start=True, stop=True),
                            reads=[q32_b[q], kmf_b], writes=[gate_b[z]])
                        P.op("dve", lambda z=z, i=i: nc.vector.tensor_tensor(
                            out=gbt[z][:], in0=gate_ps[z][:, 0:NBLK], in1=gbias[:, i, :], op=ALU.add),
                            reads=[gate_b[z], gbias_b], writes=[gbt_b[z]])
                        P.op("dve", lambda z=z: nc.vector.max(out=top8[z][:], in_=gbt[z][:]),
                             reads=[gbt_b[z]], writes=[top8_b[z]])
                        P.op("dve", lambda z=z: nc.vector.tensor_scalar(
                            out=top8[z][:, 2:3], in0=top8[z][:, 2:3], scalar1=-1.0e29, scalar2=None,
                            op0=ALU.max), reads=[top8_b[z]], writes=[top8_b[z]])
                        P.op("dve", lambda z=z, ug=ug, m=m: nc.vector.tensor_scalar(
                            out=maskS[:, ug, m, :], in0=gbt[z][:], scalar1=top8[z][:, 2:3],
                            scalar2=None, op0=ALU.is_ge), reads=[gbt_b[z], top8_b[z]],
                            writes=[maskS_b[m][i]])
            P.flush()

        P.barrier()
        st = ExitStack()
        with st:
            def sb(name, shape, dt):
                return st.enter_context(nc.sbuf_tensor(f"{tag}B_{name}", shape, dt))

            def psum(name, shape, dt=F32):
                return st.enter_context(nc.psum_tensor(f"{tag}B_{name}", shape, dt))
            Kh = sb("Kh", [128, NB, 4, MOBA_BLOCK], BF16)
            Kh_b = Buf("Kh")
            Vh = sb("Vh", [128, NB, 4, 2, HEAD_DIM + 1], BF16)
            Vh_b = Buf("Vh")
            Kl = sb("Kl", [128, TOK_PER_CORE], BF16)
            Kl_b = Buf("Kl")
            Vl = sb("Vl", [128, 2 * NB, HEAD_DIM + 1], BF16)
            Vl_b = Buf("Vl")
            tri = sb("tri", [128, 128], BF16)
            cst_b = Buf("cstB")
            NS = 2
            pt = [sb(f"pt{s}", [128, 2, MOBA_BLOCK], BF16) for s in range(NS)]
            pt_b = [Buf(f"pt{s}") for s in range(NS)]
            sp_ps = [psum(f"sp{s}", [128, 2, MOBA_BLOCK]) for s in range(NS)]
            sp_b = [Buf(f"sp{s}") for s in range(NS)]
            o_ps = [[psum(f"o{s}_{u}", [128, 512]) for u in range(2)] for s in range(NS)]
            o_b = [[Buf(f"o{s}_{u}") for u in range(2)] for s in range(NS)]
            tp_full = psum("tp", [128, 1024], BF16)
            tp_ps = tp_full[:, 0:256].rearrange("p (u q) -> p u q", u=2)
            tp_b = Buf("tp")
            mask = [sb(f"mask{s}", [128, 2, NBLK], F32) for s in range(2)]
            mask_b = [Buf(f"mask{s}") for s in range(2)]
            Oacc = [sb(f"Oacc{s}", [128, 2, HEAD_DIM + 1], F32) for s in range(2)]
            Oacc_b = [[Buf(f"Oacc{s}_{u}") for u in range(2)] for s in range(2)]
            rden = [sb(f"rden{s}", [128, 2], F32) for s in range(2)]
            rden_b = [Buf(f"rden{s}") for s in range(2)]
            obf = [sb(f"obf{s}", [128, 2, 128], BF16) for s in range(2)]
            obf_b = [Buf(f"obf{s}") for s in range(2)]

            P.dma("pool", tri[:], tri_d, writes=[cst_b])
            P.op("dve", lambda: nc.vector.memset(Vh[:].rearrange("p i r s c -> p (i r s) c")[:, :, HEAD_DIM:HEAD_DIM + 1], 1.0),
                 writes=[Vh_b])
            P.op("dve", lambda: nc.vector.memset(Vl[:, :, HEAD_DIM:HEAD_DIM + 1], 1.0), writes=[Vl_b])
            it = 0
            for hh in range(heads):
                for r in range(4):
                    P.dma("sp", Kh[:, :, r, :], kT_g[r, hh].rearrange("p (i t) -> p i t", t=MOBA_BLOCK),
                          reads=[kvg_b["k"][hh]] if kvg_b else [], writes=[Kh_b])
                    for s2 in range(2):
                        P.dma("sp", Vh[:, :, r, s2, 0:HEAD_DIM],
                              v_g[r, hh].rearrange("p (i s) c -> p i s c", s=2)[:, :, s2, :],
                              reads=[kvg_b["v"][hh]] if kvg_b else [], writes=[Vh_b])
                P.dma("sp", Kl[:], kT_l[hh], reads=[kvl_b] if kvl_b else [], writes=[Kl_b])
                for t8 in range(0, 2 * NB, 8):
                    P.dma("sp", Vl[:, t8:t8 + 8, 0:HEAD_DIM], v_l[hh][:, t8:t8 + 8, :],
                          reads=[kvl_b] if kvl_b else [], writes=[Vl_b])
                for i in range(NB):
                    qs = i * MOBA_BLOCK
                    qb = QT_b[hh][i]
                    w = i % 2
                    P.op("dve", lambda w=w, i=i, hh=hh: nc.vector.tensor_copy(
                        out=mask[w][:], in_=maskS[:, 2 * i:2 * i + 2, hh, :]),
                        reads=[maskS_b[hh][i]], writes=[mask_b[w]])
                    def qk_step(blk, z, i=i, qs=qs, hh=hh, qb=qb):
                        if blk < 0:
                            P.op("pe", lambda: nc.tensor.matmul(
                                sp_ps[z][:, 0, :], lhsT=Kl[:, qs:qs + 128], rhs=QT[:, hh, qs:qs + 256],
                                start=True, stop=True), reads=[Kl_b, qb], writes=[sp_b[z]])
                            P.op("pe", lambda: nc.tensor.matmul(
                                sp_ps[z][:, 1, 128:256], lhsT=Kl[:, qs + 128:qs + 256],
                                rhs=QT[:, hh, qs + 128:qs + 256], start=True, stop=True),
                                reads=[Kl_b, qb], writes=[sp_b[z]])
                            P.op("act", lambda: nc.scalar.activation(
                                out=pt[z][:, 0, :], in_=sp_ps[z][:, 0, :], func=AF.Exp, scale=SCALE),
                                reads=[sp_b[z]], writes=[pt_b[z]])
                            P.op("act", lambda: nc.scalar.activation(
                                out=pt[z][:, 1, 128:256], in_=sp_ps[z][:, 1, 128:256], func=AF.Exp,
                                scale=SCALE), reads=[sp_b[z]], writes=[pt_b[z]])
                            P.op("pool", lambda: nc.gpsimd.tensor_tensor(
                                out=pt[z][:, 0, 0:128], in0=pt[z][:, 0, 0:128], in1=tri[:], op=ALU.mult),
                                reads=[pt_b[z], cst_b], writes=[pt_b[z]])
                            P.op("pool", lambda: nc.gpsimd.tensor_tensor(
                                out=pt[z][:, 1, 128:256], in0=pt[z][:, 1, 128:256], in1=tri[:], op=ALU.mult),
                                reads=[pt_b[z], cst_b], writes=[pt_b[z]])
                        else:
                            for a in range(2):
                                P.op("pe", lambda a=a: nc.tensor.matmul(
                                    sp_ps[z][:, a, :], lhsT=Kh[:, blk // 4, blk % 4, a * 128:(a + 1) * 128],
                                    rhs=QT[:, hh, qs:qs + 256], start=True, stop=True),
                                    reads=[Kh_b, qb], writes=[sp_b[z]])
                            P.op("act", lambda: nc.scalar.activation(
                                out=pt[z][:], in_=sp_ps[z][:], func=AF.Exp, scale=SCALE),
                                reads=[sp_b[z]], writes=[pt_b[z]])

                    def pv_step(blk, z, i=i, w=w):
                        if blk < 0:
                            P.op("pe", lambda: nc.tensor.matmul(
                                o_ps[z][0][:, 0:HEAD_DIM + 1], lhsT=pt[z][:, 0, 0:128], rhs=Vl[:, 2 * i, :],
                                start=True, stop=True), reads=[pt_b[z], Vl_b], writes=[o_b[z][0]])
                            P.op("pe", lambda: nc.tensor.matmul(
                                o_ps[z][1][:, 0:HEAD_DIM + 1], lhsT=pt[z][:, 0, 128:256], rhs=Vl[:, 2 * i, :],
                                start=True, stop=False), reads=[pt_b[z], Vl_b], writes=[o_b[z][1]])
                            P.op("pe", lambda: nc.tensor.matmul(
                                o_ps[z][1][:, 0:HEAD_DIM + 1], lhsT=pt[z][:, 1, 128:256],
                                rhs=Vl[:, 2 * i + 1, :], start=False, stop=True),
                                reads=[pt_b[z], Vl_b], writes=[o_b[z][1]])
                            for u in range(2):
                                P.op("dve", lambda u=u: nc.vector.tensor_copy(
                                    out=Oacc[w][:, u, :], in_=o_ps[z][u][:, 0:HEAD_DIM + 1]),
                                    reads=[o_b[z][u]], writes=[Oacc_b[w][u]])
                        else:
                            for u in range(2):
                                for a in range(2):
                                    P.op("pe", lambda a=a, u=u: nc.tensor.matmul(
                                        o_ps[z][u][:, 0:HEAD_DIM + 1], lhsT=pt[z][:, a, u * 128:(u + 1) * 128],
                                        rhs=Vh[:, blk // 4, blk % 4, a, :], start=(a == 0), stop=(a == 1)),
                                        reads=[pt_b[z], Vh_b], writes=[o_b[z][u]])
                                P.op("dve", lambda u=u: nc.vector.scalar_tensor_tensor(
                                    out=Oacc[w][:, u, :], in0=o_ps[z][u][:, 0:HEAD_DIM + 1],
                                    scalar=mask[w][:, u, blk:blk + 1], in1=Oacc[w][:, u, :],
                                    op0=ALU.mult, op1=ALU.add),
                                    reads=[o_b[z][u], mask_b[w], Oacc_b[w][u]], writes=[Oacc_b[w][u]])

                    steps = [-1] + list(range(4 * i + 3))
                    zs = []
                    for blk in steps:
                        zs.append(it % NS)
                        it += 1
                    qk_step(steps[0], zs[0])
                    for si in range(1, len(steps)):
                        qk_step(steps[si], zs[si])
                        pv_step(steps[si - 1], zs[si - 1])
                    pv_step(steps[-1], zs[-1])
                    P.op("dve", lambda w=w: nc.vector.reciprocal(out=rden[w][:], in_=Oacc[w][:, :, HEAD_DIM]),
                         reads=Oacc_b[w], writes=[rden_b[w]])
                    for u in range(2):
                        P.op("dve", lambda w=w, u=u: nc.vector.tensor_scalar(
                            out=obf[w][:, u, :], in0=Oacc[w][:, u, 0:HEAD_DIM], scalar1=rden[w][:, u:u + 1],
                            scalar2=None, op0=ALU.mult), reads=[Oacc_b[w][u], rden_b[w]], writes=[obf_b[w]])
                    for u in range(2):
                        P.op("pe", lambda w=w, u=u: nc.tensor.transpose(
                            out=tp_ps[:, u, :], in_=obf[w][:, u, :], identity=cx.ident),
                            reads=[obf_b[w], cx.const_b], writes=[tp_b])
                    P.op("act", lambda qs=qs, hh=hh: nc.scalar.copy(
                        out=QT[:, hh, qs:qs + 256], in_=tp_full[:, 0:256]),
                        reads=[tp_b], writes=[qb])
            P.flush()

        P.barrier()
        st = ExitStack()
        with st:
            def sb(name, shape, dt):
                return st.enter_context(nc.sbuf_tensor(f"{tag}C_{name}", shape, dt))
            wo = sb("wo", [128, KC, D_MODEL], BF16)
            wo_b = [Buf(f"wo{k}") for k in range(KC)]
            xt = [sb(f"xt{s}", [128, KC, N], F32) for s in range(2)]
            xt_b = [[Buf(f"xt{s}_{k}") for k in range(KC)] for s in range(2)]
            y_ps = [st.enter_context(nc.psum_tensor(f"{tag}C_y{s}", [128, N], F32)) for s in range(2)]
            y_b = [Buf(f"y{s}") for s in range(2)]
            w_v = w_o_d.rearrange("(kc p) f -> p kc f", p=128)
            for k in range(KC):
                P.dma("pool", wo[:, k, :], w_v[:, k, :], writes=[wo_b[k]])

            def load(t):
                s = t % 2
                P.dma("sp", xt[s][:], x_v[:, :, t * N:(t + 1) * N], reads=[in_b] if in_b else [],
                      writes=xt_b[s])
            load(0)
            for t in range(NT):
                s = t % 2
                if t + 1 < NT:
                    load(t + 1)
                X, Xb = xt[s], xt_b[s]
                for m in range(KC):
                    q = m % 2
                    for hh in range(NH):
                        P.op("pe", lambda m=m, hh=hh, q=q, t=t: nc.tensor.matmul(
                            y_ps[q][:], lhsT=wo[:, hh, m * 128:(m + 1) * 128],
                            rhs=QT[:, hh, t * N:(t + 1) * N], start=(hh == 0), stop=(hh == NH - 1)),
                            reads=[wo_b[hh], QT_b[hh][2 * t], QT_b[hh][2 * t + 1]], writes=[y_b[q]])
                    P.op("dve", lambda X=X, m=m, q=q: nc.vector.scalar_tensor_tensor(
                        out=X[:, m, :], in0=y_ps[q][:], scalar=g1_ap[:, m:m + 1], in1=X[:, m, :],
                        op0=ALU.mult, op1=ALU.add), reads=[y_b[q], Xb[m], mod_b], writes=[Xb[m]])
                P.dma("sp", xo_v[:, :, t * N:(t + 1) * N], X[:], reads=Xb,
                      writes=[out_b] if out_b else [])
            P.flush()


import ml_dtypes

NPBF16 = ml_dtypes.bfloat16


def _din(nc, name, shape, dt=F32):
    return nc.dram_tensor(name, list(shape), dt, kind="ExternalInput").ap()


def _dout(nc, name, shape, dt=F32):
    return nc.dram_tensor(name, list(shape), dt, kind="ExternalOutput").ap()


def _new():
    nc = bass.Bass("TRN2", target_bir_lowering=False)
    return nc


def _begin(nc, stack):
    P = Prog(nc, stack)
    cx = Ctx()
    cx.nc, cx.prog = nc, P
    consts_d = _din(nc, "consts", [128, 385])
    setup_consts(cx, stack, consts_d)
    return cx, P


def _load_mod(cx, stack, modd_d, s):
    nc, P = cx.nc, cx.prog
    t = stack.enter_context(nc.sbuf_tensor("modd_sb", [128, 24], F32))
    b = Buf("modd")
    P.dma("sp", t[:], modd_d[:, s * 24:(s + 1) * 24], writes=[b])
    return (t, b)


def build_mods():
    nc = _new()
    with ExitStack() as stack:
        cx, P = _begin(nc, stack)
        cT = _din(nc, "cT", [128, 8])
        ada_w = _din(nc, "ada_w", [2, D_MODEL, 9 * D_MODEL])
        ada_b = _din(nc, "ada_b", [128, NMODV * 8])
        kvw = _din(nc, "kvw", [D_MODEL, 2 * D_MODEL])
        ng = _din(nc, "ng", [128, 56])
        psc = _din(nc, "psc", [128, 8])
        out = _dout(nc, "modd", [128, MODD_COLS])
        modd = stack.enter_context(nc.sbuf_tensor("modd_sb", [128, MODD_COLS], F32))
        modd_b = Buf("modd")
        ob = Buf("out")
        P.final_bufs.append(ob)
        mods_phase(cx, cT, ada_w, ada_b, kvw, ng, psc, modd, modd_b)
        P.dma("sp", out, modd[:], reads=[modd_b], writes=[ob])
        P.finish()
    return nc


def build_ffn(s, with_halo):
    nc = _new()
    with ExitStack() as stack:
        cx, P = _begin(nc, stack)
        modd_d = _din(nc, "modd", [128, MODD_COLS])
        mod = _load_mod(cx, stack, modd_d, s)
        x = _din(nc, "x", [D_MODEL, TOK_PER_CORE])
        y = _dout(nc, "y", [D_MODEL, TOK_PER_CORE])
        w_in = _din(nc, "w_in", [D_MODEL, 2 * D_FF])
        w_out = _din(nc, "w_out", [D_FF, D_MODEL])
        ob = Buf("out")
        P.final_bufs.append(ob)
        segs = [(x, y, TOK_PER_CORE, None, ob)]
        if with_halo:
            xh = _din(nc, "xh", [D_MODEL, BLK_PER_CORE * HALO])
            yh = _dout(nc, "yh", [D_MODEL, BLK_PER_CORE * HALO])
            segs.append((xh, yh, BLK_PER_CORE * HALO, None, ob))
        ffn_phase(cx, segs, w_in, w_out, mod, D_MODEL, D_FF)
        P.finish()
    return nc


def build_pool():
    nc = _new()
    with ExitStack() as stack:
        cx, P = _begin(nc, stack)
        modd_d = _din(nc, "modd", [128, MODD_COLS])
        mod = _load_mod(cx, stack, modd_d, 1)
        x = _din(nc, "x", [D_MODEL, TOK_PER_CORE])
        xh = _din(nc, "xh", [D_MODEL, BLK_PER_CORE * HALO])
        y = _dout(nc, "y", [D_MODEL, TOK_PER_CORE])
        wp = _din(nc, "w_pool", [4, 256, 256])
        hm = _din(nc, "hmask", [128, 1])
        ct = _din(nc, "cnt", [128, 8, HALO])
        ob = Buf("out")
        P.final_bufs.append(ob)
        pool_phase(cx, x, xh, y, wp, hm, ct, mod, None, None, ob)
        P.finish()
    return nc


def build_kv():
    nc = _new()
    with ExitStack() as stack:
        cx, P = _begin(nc, stack)
        modd_d = _din(nc, "modd", [128, MODD_COLS])
        mod = _load_mod(cx, stack, modd_d, 6)
        x = _din(nc, "x", [D_MODEL, TOK_PER_CORE])
        w_kv = _din(nc, "w_kv", [D_MODEL, 2 * D_MODEL])
        kg = _din(nc, "kg", [128, 1])
        cos = _din(nc, "cos", [128, TOK_PER_CORE])
        sin = _din(nc, "sin", [128, TOK_PER_CORE])
        kT = _dout(nc, "kT", [N_HEADS, 128, TOK_PER_CORE], BF16)
        v = _dout(nc, "v", [N_HEADS, 128, 2 * BLK_PER_CORE, HEAD_DIM + 1], BF16)
        km = _dout(nc, "km", [128, N_HEADS, BLK_PER_CORE])
        ob = Buf("out")
        P.final_bufs.append(ob)
        kv_phase(cx, x, None, w_kv, kg, cos, sin, mod, kT, v, km, ob)
        P.finish()
    return nc


def build_attn(heads=N_HEADS):
    nc = _new()
    with ExitStack() as stack:
        cx, P = _begin(nc, stack)
        modd_d = _din(nc, "modd", [128, MODD_COLS])
        mod = _load_mod(cx, stack, modd_d, 4)
        x = _din(nc, "x", [D_MODEL, TOK_PER_CORE])
        y = _dout(nc, "y", [D_MODEL, TOK_PER_CORE])
        w_q = _din(nc, "w_q", [D_MODEL, D_MODEL])
        w_o = _din(nc, "w_o", [D_MODEL, D_MODEL])
        qg = _din(nc, "qg", [128, 1])
        cos = _din(nc, "cos", [128, TOK_PER_CORE])
        sin = _din(nc, "sin", [128, TOK_PER_CORE])
        kT_g = _din(nc, "kT_g", [4, N_HEADS, 128, TOK_PER_CORE], BF16)
        v_g = _din(nc, "v_g", [4, N_HEADS, 128, 2 * BLK_PER_CORE, HEAD_DIM + 1], BF16)
        km_g = _din(nc, "km_g", [4, 128, N_HEADS, BLK_PER_CORE])
        kT_l = _din(nc, "kT_l", [N_HEADS, 128, TOK_PER_CORE], BF16)
        v_l = _din(nc, "v_l", [N_HEADS, 128, 2 * BLK_PER_CORE, HEAD_DIM + 1], BF16)
        gbias = _din(nc, "gbias", [128, BLK_PER_CORE, NBLK])
        tri = _din(nc, "tri", [128, 128])
        ob = Buf("out")
        P.final_bufs.append(ob)
        attn_phase(cx, x, None, y, ob, w_q, w_o, qg, cos, sin, mod, kT_g, v_g, km_g, kT_l, v_l,
                   None, None, gbias, tri, heads=heads)
        P.finish()
    return nc


def _fm(v):
    return np.ascontiguousarray(np.asarray(v, np.float32).reshape(8, 128).T)


def core_tokens(c):
    j = c % CORES_PER_SEQ
    blocks = 4 * np.arange(BLK_PER_CORE) + j
    return (blocks[:, None] * MOBA_BLOCK + np.arange(MOBA_BLOCK)[None, :]).reshape(-1)


def host_tables(c):
    j = c % CORES_PER_SEQ
    pos = core_tokens(c).astype(np.float32)
    inv = (np.float32(ROPE_THETA) ** (-(np.arange(0, HEAD_DIM, 2, dtype=np.float32)) / np.float32(HEAD_DIM))).astype(np.float32)
    ang = (pos[None, :] * inv[:, None]).astype(np.float32)
    cos = np.cos(ang).astype(np.float32)
    sin = np.sin(ang).astype(np.float32)
    cosT = np.concatenate([cos, cos], axis=0)
    sinS = np.concatenate([-sin, sin], axis=0)
    gbias = np.zeros((128, BLK_PER_CORE, NBLK), np.float32)
    for i in range(BLK_PER_CORE):
        gbias[:, i, 4 * i + j:] = NEG
    hmask = np.full((128, 1), 0.0 if j == 0 else 1.0, np.float32)
    cnt = np.zeros((128, 8, HALO), np.float32)
    for k in range(8):
        w = POOL_WINDOWS[k // 2]
        for t in range(HALO):
            cnt[:, k, t] = 1.0 / (min(t + 1, w) if j == 0 else w)
    return dict(cos=np.ascontiguousarray(cosT), sin=np.ascontiguousarray(sinS), gbias=gbias,
                hmask=hmask, cnt=cnt)


_NC_CACHE = {}


def _get(name, fn, *a):
    key = (name,) + a
    if key not in _NC_CACHE:
        _NC_CACHE[key] = fn(*a)
    return _NC_CACHE[key]


def _run(nc, in_maps):
    res = run_bass_kernel_spmd(nc, in_maps, core_ids=list(range(N_CORES)))
    return res.results


def kernel_unfused(x, c, ada_w, ada_b, norm_g, ffn_w_in, ffn_w_out, pool_w, pool_scale,
                   kv_norm, kv_ada_w, kv_ada_b, w_kv, k_norm, w_q, q_norm, w_o, _debug=None):
    f32 = np.float32
    x = np.asarray(x, f32)
    c = np.asarray(c, f32)
    ada_w = np.ascontiguousarray(np.asarray(ada_w, f32))
    ada_b = np.asarray(ada_b, f32)
    norm_g = np.asarray(norm_g, f32)
    ffn_w_in = np.asarray(ffn_w_in, f32)
    ffn_w_out = np.asarray(ffn_w_out, f32)
    pool_w = np.ascontiguousarray(np.asarray(pool_w, f32)[0])
    pool_scale = np.asarray(pool_scale, f32)[0]
    kv_norm = np.asarray(kv_norm, f32)
    kv_ada_w = np.ascontiguousarray(np.asarray(kv_ada_w, f32))
    kv_ada_b = np.asarray(kv_ada_b, f32)
    w_kv = np.ascontiguousarray(np.asarray(w_kv, f32))
    k_norm = np.asarray(k_norm, f32)
    w_q = np.ascontiguousarray(np.asarray(w_q, f32)[0])
    q_norm = np.asarray(q_norm, f32)[0]
    w_o = np.ascontiguousarray(np.asarray(w_o, f32)[0])
    consts = host_consts()
    cores = list(range(N_CORES))
    tabs = [host_tables(cc) for cc in cores]
    toks = [core_tokens(cc) for cc in cores]

    ada_b_l = np.concatenate([_fm(ada_b[l, v * 1024:(v + 1) * 1024]) for l in range(2) for v in range(9)]
                             + [_fm(kv_ada_b[v * 1024:(v + 1) * 1024]) for v in range(2)], axis=1)
    ng_l = np.concatenate([_fm(norm_g[l, s]) for l in range(2) for s in range(3)] + [_fm(kv_norm)], axis=1)
    psc_l = _fm(pool_scale)
    xT, xh = [], []
    for cc in cores:
        b, j = cc // 4, cc % 4
        xT.append(np.ascontiguousarray(x[b, toks[cc]].T))
        hal = np.zeros((BLK_PER_CORE * HALO, D_MODEL), f32)
        for i in range(BLK_PER_CORE):
            t0 = (4 * i + j) * MOBA_BLOCK
            if t0 > 0:
                hal[i * HALO:(i + 1) * HALO] = x[b, t0 - HALO:t0]
        xh.append(np.ascontiguousarray(hal.T))

    dbg = {} if _debug is not None else None
    r = _run(_get("mods", build_mods), [dict(consts=consts, cT=_fm(c[cc // 4]), ada_w=ada_w, ada_b=ada_b_l,
                                              kvw=kv_ada_w, ng=ng_l, psc=psc_l) for cc in cores])
    modd = [r[cc]["modd"] for cc in cores]
    wi, wo_ = np.ascontiguousarray(ffn_w_in[0, 0]), np.ascontiguousarray(ffn_w_out[0, 0])
    r = _run(_get("ffn", build_ffn, 0, True), [dict(consts=consts, modd=modd[cc], x=xT[cc], xh=xh[cc],
                                                    w_in=wi, w_out=wo_) for cc in cores])
    x1 = [r[cc]["y"] for cc in cores]
    x1h = [r[cc]["yh"] for cc in cores]
    r = _run(_get("pool", build_pool), [dict(consts=consts, modd=modd[cc], x=x1[cc], xh=x1h[cc], w_pool=pool_w,
                                              hmask=tabs[cc]["hmask"], cnt=tabs[cc]["cnt"]) for cc in cores])
    x2 = [r[cc]["y"] for cc in cores]
    wi, wo_ = np.ascontiguousarray(ffn_w_in[0, 1]), np.ascontiguousarray(ffn_w_out[0, 1])
    r = _run(_get("ffn", build_ffn, 2, False), [dict(consts=consts, modd=modd[cc], x=x2[cc], w_in=wi, w_out=wo_)
                                                 for cc in cores])
    x3 = [r[cc]["y"] for cc in cores]
    kg = np.ascontiguousarray(k_norm.reshape(128, 1))
    r = _run(_get("kv", build_kv), [dict(consts=consts, modd=modd[cc], x=x3[cc], w_kv=w_kv, kg=kg,
                                          cos=tabs[cc]["cos"], sin=tabs[cc]["sin"]) for cc in cores])
    kT_l = [r[cc]["kT"] for cc in cores]
    v_l = [r[cc]["v"] for cc in cores]
    km_l = [r[cc]["km"] for cc in cores]
    wi, wo_ = np.ascontiguousarray(ffn_w_in[1, 0]), np.ascontiguousarray(ffn_w_out[1, 0])
    r = _run(_get("ffn", build_ffn, 3, False), [dict(consts=consts, modd=modd[cc], x=x3[cc], w_in=wi, w_out=wo_)
                                                 for cc in cores])
    x4 = [r[cc]["y"] for cc in cores]
    qg = np.ascontiguousarray(q_norm.reshape(128, 1))
    tri = np.triu(np.ones((128, 128), f32))
    ins = []
    for cc in cores:
        b = cc // 4
        grp = [4 * b + rr for rr in range(4)]
        ins.append(dict(consts=consts, modd=modd[cc], x=x4[cc], w_q=w_q, w_o=w_o, qg=qg,
                        cos=tabs[cc]["cos"], sin=tabs[cc]["sin"],
                        kT_g=np.stack([kT_l[g] for g in grp]), v_g=np.stack([v_l[g] for g in grp]),
                        km_g=np.stack([km_l[g] for g in grp]), kT_l=kT_l[cc], v_l=v_l[cc],
                        gbias=tabs[cc]["gbias"], tri=tri))
    r = _run(_get("attn", build_attn), ins)
    x5 = [r[cc]["y"] for cc in cores]
    wi, wo_ = np.ascontiguousarray(ffn_w_in[1, 1]), np.ascontiguousarray(ffn_w_out[1, 1])
    r = _run(_get("ffn", build_ffn, 5, False), [dict(consts=consts, modd=modd[cc], x=x5[cc], w_in=wi, w_out=wo_)
                                                 for cc in cores])
    x6 = [r[cc]["y"] for cc in cores]
    out = np.empty((BATCH, SEQ, D_MODEL), f32)
    for cc in cores:
        out[cc // 4, toks[cc]] = x6[cc].T
    if _debug is not None:
        _debug.update(modd=modd, x1=x1, x1h=x1h, x2=x2, x3=x3, kT=kT_l, v=v_l, km=km_l, x4=x4, x5=x5, x6=x6,
                      toks=toks)
    return out


GROUPS = [[0, 1, 2, 3], [4, 5, 6, 7]]


def build_fused(stop_after=99):
    nc = _new()
    NH, T, NT128 = N_HEADS, TOK_PER_CORE, 2 * BLK_PER_CORE
    with ExitStack() as stack:
        cx, P = _begin(nc, stack)
        x = _din(nc, "x", [D_MODEL, T])
        xh = _din(nc, "xh", [D_MODEL, BLK_PER_CORE * HALO])
        cT = _din(nc, "cT", [128, 8])
        ada_w = _din(nc, "ada_w", [2, D_MODEL, 9 * D_MODEL])
        ada_b = _din(nc, "ada_b", [128, NMODV * 8])
        kvw = _din(nc, "kvw", [D_MODEL, 2 * D_MODEL])
        ng = _din(nc, "ng", [128, 56])
        psc = _din(nc, "psc", [128, 8])
        w_in = _din(nc, "w_in", [2, 2, D_MODEL, 2 * D_FF])
        w_out = _din(nc, "w_out", [2, 2, D_FF, D_MODEL])
        wp = _din(nc, "w_pool", [4, 256, 256])
        hm = _din(nc, "hmask", [128, 1])
        ct = _din(nc, "cnt", [128, 8, HALO])
        w_kv = _din(nc, "w_kv", [D_MODEL, 2 * D_MODEL])
        kg = _din(nc, "kg", [128, 1])
        cos = _din(nc, "cos", [128, T])
        sin = _din(nc, "sin", [128, T])
        w_q = _din(nc, "w_q", [D_MODEL, D_MODEL])
        w_o = _din(nc, "w_o", [D_MODEL, D_MODEL])
        qg = _din(nc, "qg", [128, 1])
        gbias = _din(nc, "gbias", [128, BLK_PER_CORE, NBLK])
        tri = _din(nc, "tri", [128, 128])
        y = _dout(nc, "y", [D_MODEL, T])

        def scr(name, shape, dt=F32):
            return nc.dram_tensor(name, list(shape), dt, kind="Internal").ap()
        x1 = scr("x1", [D_MODEL, T])
        x1h = scr("x1h", [D_MODEL, BLK_PER_CORE * HALO])
        x2 = scr("x2", [D_MODEL, T])
        x3 = scr("x3", [D_MODEL, T])
        x4 = scr("x4", [D_MODEL, T])
        x5 = scr("x5", [D_MODEL, T])
        kT2 = scr("kT_l", [NH * 128, T], BF16)
        v2 = scr("v_l", [NH * 128, NT128 * HEAD_DIM], BF16)
        km2 = scr("km_l", [128, NH * BLK_PER_CORE])
        kTg2 = scr("kT_g", [NH, 4 * 128, T], BF16)
        vg2 = scr("v_g", [NH, 4 * 128, NT128 * HEAD_DIM], BF16)
        kmg2 = scr("km_g", [4 * 128, NH * BLK_PER_CORE])
        kT_l = kT2.rearrange("(h p) n -> h p n", p=128)
        v_l = v2.rearrange("(h p) (t c) -> h p t c", p=128, c=HEAD_DIM)
        km_l = km2.rearrange("p (h i) -> p h i", i=BLK_PER_CORE)
        kT_g = kTg2.rearrange("h (r p) n -> r h p n", r=4)
        v_g = vg2.rearrange("h (r p) (t c) -> r h p t c", r=4, c=HEAD_DIM)
        km_g = kmg2.rearrange("(r p) (h i) -> r p h i", p=128, i=BLK_PER_CORE)
        b = {n: Buf(n) for n in "x1 x1h x2 x3 x4 x5 kvl out".split()}
        kvg = {"km": Buf("kmg"), "k": [Buf(f"kg{h}") for h in range(NH)],
               "v": [Buf(f"vg{h}") for h in range(NH)]}
        P.final_bufs.append(b["out"])

        modd = stack.enter_context(nc.sbuf_tensor("modd_sb", [128, MODD_COLS], F32))
        modd_b = Buf("modd")

        def mod(s):
            return (modd[:, s * 24:(s + 1) * 24], modd_b)

        def early(src_ap, src_b):
            P.dma("sp", y, src_ap, reads=[src_b], writes=[b["out"]])
            P.finish()

        mods_phase(cx, cT, ada_w, ada_b, kvw, ng, psc, modd, modd_b)
        ffn_phase(cx, [(x, x1, T, None, b["x1"]), (xh, x1h, BLK_PER_CORE * HALO, None, b["x1h"])],
                  w_in[0, 0], w_out[0, 0], mod(0), D_MODEL, D_FF, tag="f0")
        if stop_after == 1:
            early(x1, b["x1"])
            return nc
        pool_phase(cx, x1, x1h, x2, wp, hm, ct, mod(1), b["x1"], b["x1h"], b["x2"])
        ffn_phase(cx, [(x2, x3, T, b["x2"], b["x3"])], w_in[0, 1], w_out[0, 1], mod(2), D_MODEL, D_FF,
                  tag="f1")
        if stop_after == 3:
            early(x3, b["x3"])
            return nc
        kv_phase(cx, x3, b["x3"], w_kv, kg, cos, sin, mod(6), kT_l, v_l, km_l, b["kvl"])
        if stop_after == 4:
            early(x3, b["kvl"])
            return nc
        P.coll("AllGather", GROUPS, km2, kmg2, reads=[b["kvl"]], writes=[kvg["km"]])
        for hh in range(NH):
            P.coll("AllGather", GROUPS, kT2[hh * 128:(hh + 1) * 128, :], kTg2[hh], reads=[b["kvl"]],
                   writes=[kvg["k"][hh]])
            P.coll("AllGather", GROUPS, v2[hh * 128:(hh + 1) * 128, :], vg2[hh], reads=[b["kvl"]],
                   writes=[kvg["v"][hh]])
        ffn_phase(cx, [(x3, x4, T, b["x3"], b["x4"])], w_in[1, 0], w_out[1, 0], mod(3), D_MODEL, D_FF,
                  tag="f2")
        if stop_after == 5:
            P.dma("sp", x5, x4, reads=[b["x4"], kvg["km"]] + kvg["k"] + kvg["v"], writes=[b["x5"]])
            early(x5, b["x5"])
            return nc
        attn_phase(cx, x4, b["x4"], x5, b["x5"], w_q, w_o, qg, cos, sin, mod(4), kT_g, v_g, km_g,
                   kT_l, v_l, kvg, b["kvl"], gbias, tri)
        if stop_after == 6:
            early(x5, b["x5"])
            return nc
        ffn_phase(cx, [(x5, y, T, b["x5"], b["out"])], w_in[1, 1], w_out[1, 1], mod(5), D_MODEL, D_FF,
                  tag="f3")
        P.finish()
    return nc


def kernel(x, c, ada_w, ada_b, norm_g, ffn_w_in, ffn_w_out, pool_w, pool_scale,
           kv_norm, kv_ada_w, kv_ada_b, w_kv, k_norm, w_q, q_norm, w_o, _stop_after=99):
    f32 = np.float32
    x = np.asarray(x, f32)
    c = np.asarray(c, f32)
    ada_w = np.ascontiguousarray(np.asarray(ada_w, f32))
    ada_b = np.asarray(ada_b, f32)
    norm_g = np.asarray(norm_g, f32)
    ffn_w_in = np.ascontiguousarray(np.asarray(ffn_w_in, f32))
    ffn_w_out = np.ascontiguousarray(np.asarray(ffn_w_out, f32))
    pool_w = np.ascontiguousarray(np.asarray(pool_w, f32)[0])
    pool_scale = np.asarray(pool_scale, f32)[0]
    kv_norm = np.asarray(kv_norm, f32)
    kv_ada_w = np.ascontiguousarray(np.asarray(kv_ada_w, f32))
    kv_ada_b = np.asarray(kv_ada_b, f32)
    w_kv = np.ascontiguousarray(np.asarray(w_kv, f32))
    k_norm = np.asarray(k_norm, f32)
    w_q = np.ascontiguousarray(np.asarray(w_q, f32)[0])
    q_norm = np.asarray(q_norm, f32)[0]
    w_o = np.ascontiguousarray(np.asarray(w_o, f32)[0])
    consts = host_consts()
    cores = list(range(N_CORES))
    ada_b_l = np.concatenate([_fm(ada_b[l, v * 1024:(v + 1) * 1024]) for l in range(2) for v in range(9)]
                             + [_fm(kv_ada_b[v * 1024:(v + 1) * 1024]) for v in range(2)], axis=1)
    ng_l = np.concatenate([_fm(norm_g[l, s]) for l in range(2) for s in range(3)] + [_fm(kv_norm)], axis=1)
    psc_l = _fm(pool_scale)
    kg = np.ascontiguousarray(k_norm.reshape(128, 1))
    qg = np.ascontiguousarray(q_norm.reshape(128, 1))
    tri = np.triu(np.ones((128, 128), f32))
    in_maps, toks = [], []
    for cc in cores:
        bb, j = cc // 4, cc % 4
        tk = core_tokens(cc)
        toks.append(tk)
        tabs = host_tables(cc)
        hal = np.zeros((BLK_PER_CORE * HALO, D_MODEL), f32)
        for i in range(BLK_PER_CORE):
            t0 = (4 * i + j) * MOBA_BLOCK
            if t0 > 0:
                hal[i * HALO:(i + 1) * HALO] = x[bb, t0 - HALO:t0]
        in_maps.append(dict(
            consts=consts, x=np.ascontiguousarray(x[bb, tk].T), xh=np.ascontiguousarray(hal.T),
            cT=_fm(c[bb]), ada_w=ada_w, ada_b=ada_b_l, kvw=kv_ada_w, ng=ng_l, psc=psc_l,
            w_in=ffn_w_in, w_out=ffn_w_out, w_pool=pool_w, hmask=tabs["hmask"], cnt=tabs["cnt"],
            w_kv=w_kv, kg=kg, cos=tabs["cos"], sin=tabs["sin"], w_q=w_q, w_o=w_o, qg=qg,
            gbias=tabs["gbias"], tri=tri))
    r = _run(_get("fused", build_fused, _stop_after), in_maps)
    out = np.empty((BATCH, SEQ, D_MODEL), f32)
    for cc in cores:
        out[cc // 4, toks[cc]] = r[cc]["y"].T
    return out
```
